# Optimizing a Trainium2 kernel written in Bass

```python
import jax, jax.numpy as jnp
from jax import lax
import numpy as np

D_MODEL = 1024
BATCH = 8
SEQ = 4096
DEPTH = 4

N_MIXERS = 3
N_META = 16
NORM_EPS = 1e-6

FOX_HEADS = 16
FOX_HEAD_DIM = D_MODEL // FOX_HEADS
FOX_Q_BLOCK = 128
FOX_IN = 4 * D_MODEL + FOX_HEADS

GLA_HEADS = 4
GLA_DK = D_MODEL // 2 // GLA_HEADS
GLA_DV = D_MODEL // GLA_HEADS
GLA_GATE_RANK = 16
GLA_GATE_NORMALIZER = 16.0
GLA_CHUNK = 64
GLA_QK = GLA_HEADS * GLA_DK
GLA_V = GLA_HEADS * GLA_DV
GLA_IN = 2 * GLA_QK + 2 * GLA_V + GLA_GATE_RANK

GDN_HEADS = 8
GDN_DK = 128
GDN_DV = 128
GDN_CONV = 4
GDN_CHUNK = 64
GDN_CONV_DIM = 2 * GDN_HEADS * GDN_DK + GDN_HEADS * GDN_DV
GDN_IN = GDN_CONV_DIM + GDN_HEADS * GDN_DV + 2 * GDN_HEADS

D_FF = ((-(-8 * D_MODEL // 3) + 255) // 256) * 256

N_FOX = (DEPTH + 2) // 3
N_GLA = (DEPTH + 1) // 3
N_GDN = DEPTH // 3

kernel_name = "fox_gla_gdn_interleaved_hybrid"


def rmsnorm(x, g):
    xf = x.astype(jnp.float32)
    y = xf * lax.rsqrt(jnp.mean(xf * xf, axis=-1, keepdims=True) + NORM_EPS)
    return (y * g.astype(jnp.float32)).astype(x.dtype)


def l2norm(x):
    xf = x.astype(jnp.float32)
    return (xf * lax.rsqrt(jnp.sum(xf * xf, axis=-1, keepdims=True) + NORM_EPS)).astype(x.dtype)


def to_chunks(a, chunk):
    b, t = a.shape[0], a.shape[1]
    return jnp.moveaxis(a.reshape((b, t // chunk, chunk) + a.shape[2:]), 1, 0)


def from_chunks(a):
    a = jnp.moveaxis(a, 0, 1)
    return a.reshape((a.shape[0], a.shape[1] * a.shape[2]) + a.shape[3:])


def chunked_scan(chunk_fn, state0, seqs, chunk):
    state, o_meta = chunk_fn(state0, tuple(a[:, :N_META] for a in seqs))
    real = tuple(to_chunks(a[:, N_META:], chunk) for a in seqs)
    _, o_real = lax.scan(chunk_fn, state, real)
    return jnp.concatenate([o_meta, from_chunks(o_real)], axis=1)


def fox_attend(q, cq, qpos, k, v, ck, kpos):
    s = jnp.einsum('bqhd,bkhd->bhqk', q, k).astype(jnp.float32)
    bias = cq[..., :, None] - ck[..., None, :]
    mask = kpos[None, :] <= qpos[:, None]
    s = jnp.where(mask, s + bias, -jnp.inf)
    p = jax.nn.softmax(s, axis=-1)
    return jnp.einsum('bhqk,bkhd->bqhd', p.astype(v.dtype), v)


def fox_mixer(h, w_in, b_f, q_gain, k_gain, w_out):
    B, L, _ = h.shape
    H, Dh = FOX_HEADS, FOX_HEAD_DIM
    proj = h @ w_in
    q, k, v, gate, f = jnp.split(proj, [D_MODEL, 2 * D_MODEL, 3 * D_MODEL, 4 * D_MODEL], axis=-1)
    q = rmsnorm(q.reshape(B, L, H, Dh), q_gain) * (Dh ** -0.5)
    k = rmsnorm(k.reshape(B, L, H, Dh), k_gain)
    v = v.reshape(B, L, H, Dh)
    log_f = jax.nn.log_sigmoid((f + b_f).astype(jnp.float32))
    c = jnp.cumsum(log_f, axis=1).transpose(0, 2, 1)
    kpos = jnp.arange(L)

    o_meta = fox_attend(q[:, :N_META], c[:, :, :N_META], kpos[:N_META],
                        k[:, :N_META], v[:, :N_META], c[:, :, :N_META], kpos[:N_META])

    n_blk = (L - N_META) // FOX_Q_BLOCK
    q_blocks = q[:, N_META:].reshape(B, n_blk, FOX_Q_BLOCK, H, Dh).swapaxes(0, 1)
    cq_blocks = c[:, :, N_META:].reshape(B, H, n_blk, FOX_Q_BLOCK).transpose(2, 0, 1, 3)
    qpos_blocks = (N_META + jnp.arange(n_blk * FOX_Q_BLOCK)).reshape(n_blk, FOX_Q_BLOCK)

    def block(args):
        qb, cqb, qp = args
        return fox_attend(qb, cqb, qp, k, v, c, kpos)

    o_real = from_chunks(lax.map(block, (q_blocks, cq_blocks, qpos_blocks)))
    o = jnp.concatenate([o_meta, o_real], axis=1).reshape(B, L, H * Dh)
    o = o * jax.nn.sigmoid(gate)
    return (o @ w_out).astype(h.dtype)


def gla_chunk(state, inputs):
    q, k, v, g = inputs
    C = q.shape[1]
    b = jnp.cumsum(g, axis=1)
    causal = jnp.tril(jnp.ones((C, C), dtype=bool))
    diff = b[:, :, None] - b[:, None, :]
    decay = jnp.exp(jnp.where(causal[None, :, :, None, None], diff, -jnp.inf))
    A = jnp.einsum('bthd,bshd,btshd->bhts', q, k, decay)
    o_intra = jnp.einsum('bhts,bshv->bthv', A, v)
    o_inter = jnp.einsum('bthd,bhdv->bthv', q * jnp.exp(b), state)
    b_last = b[:, -1]
    k_dec = k * jnp.exp(b_last[:, None] - b)
    new_state = state * jnp.exp(b_last)[..., None] + jnp.einsum('bshd,bshv->bhdv', k_dec, v)
    return new_state, (o_intra + o_inter).astype(v.dtype)


def gla_mixer(h, w_in, w_alpha2, b_alpha, o_gain, w_out):
    B, L, _ = h.shape
    H = GLA_HEADS
    proj = h @ w_in
    q, k, v, r, a_lr = jnp.split(proj, [GLA_QK, 2 * GLA_QK, 2 * GLA_QK + GLA_V, 2 * GLA_QK + 2 * GLA_V], axis=-1)
    q = q.reshape(B, L, H, GLA_DK) * (GLA_DK ** -0.5)
    k = k.reshape(B, L, H, GLA_DK)
    v = v.reshape(B, L, H, GLA_DV)
    g = jax.nn.log_sigmoid((a_lr @ w_alpha2 + b_alpha).astype(jnp.float32)) / GLA_GATE_NORMALIZER
    g = g.reshape(B, L, H, GLA_DK)
    state0 = jnp.zeros((B, H, GLA_DK, GLA_DV), jnp.float32)
    o = chunked_scan(gla_chunk, state0, (q, k, v, g), GLA_CHUNK)
    o = rmsnorm(o, o_gain) * jax.nn.silu(r.reshape(B, L, H, GLA_DV))
    return (o.reshape(B, L, GLA_V) @ w_out).astype(h.dtype)


def causal_depthwise_conv(x, w):
    return lax.conv_general_dilated(
        x, w.astype(x.dtype), window_strides=(1,), padding=[(GDN_CONV - 1, 0)],
        dimension_numbers=('NWC', 'WIO', 'NWC'), feature_group_count=x.shape[-1])


def gdn_chunk(state, inputs):
    q, k, v, g, beta = inputs
    C = q.shape[1]
    qh = q.transpose(0, 2, 1, 3)
    kh = k.transpose(0, 2, 1, 3)
    vh = v.transpose(0, 2, 1, 3)
    bt = beta.transpose(0, 2, 1).astype(jnp.float32)[..., None]
    b = jnp.cumsum(g, axis=1).transpose(0, 2, 1)
    diff = b[..., :, None] - b[..., None, :]
    incl = jnp.tril(jnp.ones((C, C), dtype=bool))
    strict = jnp.tril(jnp.ones((C, C), dtype=bool), k=-1)
    kb = kh.astype(jnp.float32) * bt
    vb = vh.astype(jnp.float32) * bt
    lower = jnp.einsum('bhtd,bhsd->bhts', kb, kh) * jnp.exp(jnp.where(strict, diff, -jnp.inf))
    t_mat = jnp.eye(C, dtype=jnp.float32) + lower
    rhs = jnp.concatenate([vb, kb * jnp.exp(b)[..., None]], axis=-1)
    sol = lax.linalg.triangular_solve(t_mat, rhs, left_side=True, lower=True, unit_diagonal=True)
    u, w = sol[..., :GDN_DV], sol[..., GDN_DV:]
    v_new = u - jnp.einsum('bhcd,bhdv->bhcv', w, state)
    attn = jnp.einsum('bhtd,bhsd->bhts', qh, kh) * jnp.exp(jnp.where(incl, diff, -jnp.inf))
    o = jnp.einsum('bhts,bhsv->bhtv', attn, v_new) + \
        jnp.einsum('bhtd,bhdv->bhtv', qh * jnp.exp(b)[..., None], state)
    b_last = b[..., -1]
    k_dec = kh * jnp.exp(b_last[..., None] - b)[..., None]
    new_state = state * jnp.exp(b_last)[..., None, None] + jnp.einsum('bhsd,bhsv->bhdv', k_dec, v_new)
    return new_state, o.transpose(0, 2, 1, 3).astype(v.dtype)


def gdn_mixer(h, w_in, conv_w, a_log, dt_bias, o_gain, w_out):
    B, L, _ = h.shape
    H = GDN_HEADS
    proj = h @ w_in
    qkv, gate, a, beta_logit = jnp.split(
        proj, [GDN_CONV_DIM, GDN_CONV_DIM + H * GDN_DV, GDN_CONV_DIM + H * GDN_DV + H], axis=-1)
    qkv = jax.nn.silu(causal_depthwise_conv(qkv, conv_w))
    q, k, v = jnp.split(qkv, [H * GDN_DK, 2 * H * GDN_DK], axis=-1)
    q = l2norm(q.reshape(B, L, H, GDN_DK)) * (GDN_DK ** -0.5)
    k = l2norm(k.reshape(B, L, H, GDN_DK))
    v = v.reshape(B, L, H, GDN_DV)
    beta = jax.nn.sigmoid(beta_logit)
    g = -jnp.exp(a_log.astype(jnp.float32)) * jax.nn.softplus((a + dt_bias).astype(jnp.float32))
    state0 = jnp.zeros((B, H, GDN_DK, GDN_DV), jnp.float32)
    o = chunked_scan(gdn_chunk, state0, (q, k, v, g, beta), GDN_CHUNK)
    o = rmsnorm(o, o_gain) * jax.nn.silu(gate.reshape(B, L, H, GDN_DV))
    return (o.reshape(B, L, H * GDN_DV) @ w_out).astype(h.dtype)


def swiglu(h, w_gate_up, w_down):
    gu = h @ w_gate_up
    gt, up = gu[..., :D_FF], gu[..., D_FF:]
    return ((jax.nn.silu(gt) * up) @ w_down).astype(h.dtype)


def setup_inputs(seed: int = 0) -> dict:
    key = jax.random.key(seed)
    ks = jax.random.split(key, 24)

    def nrm(k, shape, fan_in):
        return jax.random.normal(k, shape, jnp.float32) * (fan_in ** -0.5)

    def gain(k, shape):
        return 1.0 + 0.02 * jax.random.normal(k, shape, jnp.float32)

    dt = jnp.exp(jax.random.uniform(ks[20], (N_GDN, GDN_HEADS), jnp.float32, np.log(1e-3), np.log(1e-1)))
    return {
        "x": jax.random.normal(ks[0], (BATCH, SEQ, D_MODEL), jnp.float32),
        "meta_tokens": jax.random.normal(ks[1], (N_META, D_MODEL), jnp.float32),
        "norm_mix": gain(ks[2], (DEPTH, D_MODEL)),
        "norm_ffn": gain(ks[3], (DEPTH, D_MODEL)),
        "w_gate_up": nrm(ks[4], (DEPTH, D_MODEL, 2 * D_FF), D_MODEL),
        "w_down": nrm(ks[5], (DEPTH, D_FF, D_MODEL), D_FF),
        "fox_w_in": nrm(ks[6], (N_FOX, D_MODEL, FOX_IN), D_MODEL),
        "fox_b_f": 2.0 + 0.5 * jax.random.normal(ks[7], (N_FOX, FOX_HEADS), jnp.float32),
        "fox_q_gain": gain(ks[8], (N_FOX, FOX_HEAD_DIM)),
        "fox_k_gain": gain(ks[9], (N_FOX, FOX_HEAD_DIM)),
        "fox_w_out": nrm(ks[10], (N_FOX, D_MODEL, D_MODEL), D_MODEL),
        "gla_w_in": nrm(ks[11], (N_GLA, D_MODEL, GLA_IN), D_MODEL),
        "gla_w_alpha2": nrm(ks[12], (N_GLA, GLA_GATE_RANK, GLA_QK), GLA_GATE_RANK),
        "gla_b_alpha": 0.1 * jax.random.normal(ks[13], (N_GLA, GLA_QK), jnp.float32),
        "gla_o_gain": gain(ks[14], (N_GLA, GLA_DV)),
        "gla_w_out": nrm(ks[15], (N_GLA, GLA_V, D_MODEL), GLA_V),
        "gdn_w_in": nrm(ks[16], (N_GDN, D_MODEL, GDN_IN), D_MODEL),
        "gdn_conv_w": nrm(ks[17], (N_GDN, GDN_CONV, 1, GDN_CONV_DIM), GDN_CONV),
        "gdn_a_log": jnp.log(jax.random.uniform(ks[18], (N_GDN, GDN_HEADS), jnp.float32, 1.0, 16.0)),
        "gdn_dt_bias": dt + jnp.log(-jnp.expm1(-dt)),
        "gdn_o_gain": gain(ks[19], (N_GDN, GDN_DV)),
        "gdn_w_out": nrm(ks[21], (N_GDN, GDN_HEADS * GDN_DV, D_MODEL), GDN_HEADS * GDN_DV),
    }


def reference(x, meta_tokens, norm_mix, norm_ffn, w_gate_up, w_down,
              fox_w_in, fox_b_f, fox_q_gain, fox_k_gain, fox_w_out,
              gla_w_in, gla_w_alpha2, gla_b_alpha, gla_o_gain, gla_w_out,
              gdn_w_in, gdn_conv_w, gdn_a_log, gdn_dt_bias, gdn_o_gain, gdn_w_out):
    B = x.shape[0]
    meta = jnp.broadcast_to(meta_tokens[None].astype(x.dtype), (B, N_META, D_MODEL))
    h = jnp.concatenate([meta, x], axis=1)
    for i in range(DEPTH):
        kind, j = i % N_MIXERS, i // N_MIXERS
        y = rmsnorm(h, norm_mix[i])
        if kind == 0:
            mix = fox_mixer(y, fox_w_in[j], fox_b_f[j], fox_q_gain[j], fox_k_gain[j], fox_w_out[j])
        elif kind == 1:
            mix = gla_mixer(y, gla_w_in[j], gla_w_alpha2[j], gla_b_alpha[j], gla_o_gain[j], gla_w_out[j])
        else:
            mix = gdn_mixer(y, gdn_w_in[j], gdn_conv_w[j], gdn_a_log[j], gdn_dt_bias[j], gdn_o_gain[j], gdn_w_out[j])
        h = h + mix
        h = h + swiglu(rmsnorm(h, norm_ffn[i]), w_gate_up[i], w_down[i])
    return h[:, N_META:]
```

```python
import contextlib
import numpy as np
import concourse.bass as bass
import concourse.mybir as mybir
from concourse.bass_utils import run_bass_kernel_spmd

F32 = mybir.dt.float32
BF16 = mybir.dt.bfloat16
AF = mybir.ActivationFunctionType
ALU = mybir.AluOpType
AX = mybir.AxisListType

T = 4224
NT = 33
D = 1024
DFF = 2816
NFC = 22
EPS = 1e-6
GROUPS = [(g * 4, 4) for g in range(8)] + [(32, 1)]
ENGS = ("tensor", "vector", "scalar", "gpsimd", "sync")
ARENA_WORDS = 45000
SLOTA_WORDS = 8 * 2 * 2816 // 2
WINDOW = {"tensor": 192, "vector": 48, "scalar": 48, "gpsimd": 32, "sync": 40}


class Buf:
    __slots__ = ("name", "lastw", "readers", "sem", "excl")

    def __init__(self, name, sem, lastw):
        self.name = name
        self.lastw = lastw
        self.readers = []
        self.sem = sem
        self.excl = False


class Op:
    __slots__ = ("eng", "fn", "deps", "odeps", "signal", "semkey", "val", "isdma", "cost")

    def __init__(self, eng, fn, deps, odeps, semkey, isdma, cost):
        self.eng = eng
        self.fn = fn
        self.deps = deps
        self.odeps = odeps
        self.signal = False
        self.semkey = semkey
        self.val = 0
        self.isdma = isdma
        self.cost = cost


class Tl:
    __slots__ = ("ap", "b")

    def __init__(self, ap, b):
        self.ap = ap
        self.b = b


def _fs(ap):
    sh = ap.shape
    n = 1
    for x in sh[1:]:
        n *= int(x)
    return n


def _nb(ap):
    sh = ap.shape
    n = 1
    for x in sh:
        n *= int(x)
    return n * (4 if ap.dtype == F32 else 2)


def _bufs(lst):
    return [x.b if isinstance(x, Tl) else x for x in lst]


class Prog:
    def __init__(self, nc):
        self.nc = nc
        self.ops = []
        self.nsem = 0
        self.free_sems = {"d": [], "w": []}
        self.cur_barrier = None

    def newsem(self, kind):
        if self.free_sems[kind]:
            return self.free_sems[kind].pop()
        self.nsem += 1
        return "%s%03d" % (kind, self.nsem)

    def buf(self, name, dma=False):
        kind = "w" if dma == "sw" else "d"
        return Buf(name, self.newsem(kind) if dma else None, self.cur_barrier)

    def add(self, eng, fn, reads=(), writes=(), dsem=None, cost=100.0):
        reads = _bufs(reads)
        writes = _bufs(writes)
        idx = len(self.ops)
        deps = set()
        for b in reads:
            if b.lastw is not None:
                deps.add(b.lastw)
            if b.excl:
                deps.update(r for r in b.readers if self.ops[r].eng != eng)
        for b in writes:
            if b.lastw is not None:
                deps.add(b.lastw)
            deps.update(b.readers)
        isdma = dsem is not None
        semkey = dsem if isdma else eng
        odeps = set()
        if eng == "tensor":
            odeps = {d for d in deps if self.ops[d].eng == "tensor"}
            deps = deps - odeps
        self.ops.append(Op(eng, fn, deps, odeps, semkey, isdma, cost))
        for b in reads:
            b.readers.append(idx)
        for b in writes:
            b.lastw = idx
            b.readers = []
        return idx

    def schedule(self, window):
        ops = self.ops
        n = len(ops)
        per = {e: [] for e in ENGS}
        for i, op in enumerate(ops):
            per[op.eng].append(i)
        if not window:
            return per
        finish = [None] * n
        alld = [tuple(op.deps | op.odeps) for op in ops]
        order = {e: [] for e in ENGS}
        head = {e: 0 for e in ENGS}
        free = {e: 0.0 for e in ENGS}
        done = [False] * n
        remaining = n
        ISSUE = 60.0
        while remaining:
            progressed = False
            for e in ENGS:
                lst = per[e]
                L = len(lst)
                W = window[e]
                while True:
                    h = head[e]
                    while h < L and done[lst[h]]:
                        h += 1
                    head[e] = h
                    if h >= L:
                        break
                    best = None
                    cnt = 0
                    j = h
                    fr = free[e]
                    while j < L and cnt < W:
                        i = lst[j]
                        j += 1
                        if done[i]:
                            continue
                        cnt += 1
                        r = 0.0
                        ok = True
                        for d in alld[i]:
                            f = finish[d]
                            if f is None:
                                ok = False
                                break
                            if f > r:
                                r = f
                        if not ok:
                            if ops[i].fn is None or cnt == 1 and False:
                                pass
                            continue
                        st = r if r > fr else fr
                        if best is None or st < best[0]:
                            best = (st, i)
                            if st <= fr:
                                break
                    if best is None:
                        break
                    st, i = best
                    op = ops[i]
                    if op.isdma:
                        free[e] = st + ISSUE
                        finish[i] = st + op.cost
                    else:
                        free[e] = st + op.cost
                        finish[i] = st + op.cost + 40.0
                    done[i] = True
                    order[e].append(i)
                    remaining -= 1
                    progressed = True
            assert progressed, "scheduler stuck"
        return order

    def emit(self, window=None):
        nc = self.nc
        ops = self.ops
        for op in ops:
            for d in op.deps:
                ops[d].signal = True
        per = self.schedule(window)
        counts = {}
        seq = sorted(range(len(ops)), key=lambda i: 0) if False else None
        pos = {}
        for e in ENGS:
            for i in per[e]:
                op = ops[i]
                if op.signal and not op.isdma:
                    counts[op.semkey] = counts.get(op.semkey, 0) + 1
                    op.val = counts[op.semkey]
        for e in ENGS:
            for i in per[e]:
                op = ops[i]
                if op.signal and op.isdma:
                    counts[op.semkey] = counts.get(op.semkey, 0) + 16
                    op.val = counts[op.semkey]
        keys = sorted(counts.keys())
        with contextlib.ExitStack() as st:
            sems = {k: st.enter_context(nc.semaphore("s_" + k)) for k in keys}
            block = st.enter_context(nc.Block())

            def run(engname, e):
                waited = {}
                for i in per[engname]:
                    op = ops[i]
                    need = {}
                    for d in op.deps:
                        dop = ops[d]
                        if dop.val > need.get(dop.semkey, 0):
                            need[dop.semkey] = dop.val
                    for k, v in need.items():
                        if v > waited.get(k, 0):
                            e.wait_ge(sems[k], v)
                            waited[k] = v
                    if op.fn is None:
                        continue
                    ins = op.fn(e)
                    if op.signal:
                        ins.then_inc(sems[op.semkey], 16 if op.isdma else 1)

            @block.tensor
            def _(e):
                run("tensor", e)

            @block.vector
            def _(e):
                run("vector", e)

            @block.scalar
            def _(e):
                run("scalar", e)

            @block.gpsimd
            def _(e):
                run("gpsimd", e)

            @block.sync
            def _(e):
                run("sync", e)


class MK:
    def __init__(self, nlayers=4, dbg=None, stop=None, only=None, window=None):
        self.window = WINDOW if window is None else window
        self.only = only
        self.nlayers = nlayers
        self.dbg = dbg
        self.stop = stop
        self.nc = bass.Bass("TRN2", target_bir_lowering=False)
        self.P = Prog(self.nc)
        self.live = []
        self.extra_live = []
        self.slotA_live = False
        self.off = 0

    def din(self, name, shape, dt=F32):
        return self.nc.dram_tensor(name, list(shape), dt, kind="ExternalInput").ap()

    def dscr(self, name, shape, dt):
        return self.nc.dram_tensor(name, list(shape), dt, kind="Internal").ap()

    def mark(self):
        return (self.off, len(self.live))

    def release(self, m):
        for b in self.live[m[1]:]:
            if b.sem is not None:
                self.P.free_sems[b.sem[0]].append(b.sem)
        del self.live[m[1]:]
        self.off = m[0]

    def tile(self, p, shape, dt, name="t", dma=False):
        n = int(np.prod(shape))
        words = n if dt == F32 else (n + 1) // 2
        off = self.off
        self.off += words
        lim = ARENA_WORDS - (SLOTA_WORDS if self.slotA_live else 0)
        assert self.off <= lim, ("SBUF arena overflow", name, self.off, lim)
        ap = self.SB[0:p, off:off + words]
        if dt != F32:
            ap = ap.bitcast(dt)[:, 0:n]
        if len(shape) == 2:
            ap = ap.rearrange("p (a b) -> p a b", a=shape[0])
        elif len(shape) == 3:
            ap = ap.rearrange("p (a b c) -> p a b c", a=shape[0], b=shape[1])
        b = self.P.buf(name, dma)
        self.live.append(b)
        return Tl(ap, b)

    def slot_tile(self, shape, name):
        n = int(np.prod(shape))
        words = (n + 1) // 2
        assert words <= SLOTA_WORDS and self.off <= ARENA_WORDS - SLOTA_WORDS
        off = ARENA_WORDS - SLOTA_WORDS
        ap = self.SB[0:128, off:off + words].bitcast(BF16)[:, 0:n].rearrange("p (a b) -> p a b", a=shape[0])
        b = self.P.buf(name, "sw")
        self.extra_live.append(b)
        self.slotA_live = True
        return Tl(ap, b)

    def slot_free(self, tl):
        self.extra_live.remove(tl.b)
        self.P.free_sems["w"].append(tl.b.sem)
        self.slotA_live = False

    def barrier(self):
        P = self.P
        d = self.dummy
        idx = P.add("gpsimd", lambda e: e.memset(d.ap[:, 0:1], 0.0), [], list(self.live) + list(self.extra_live), cost=1000.0)
        P.cur_barrier = idx

    def mm(self, out, lhsT, rhs, start, stop, reads, writes):
        n = _fs(rhs)
        c = max(n / 2.4, _fs(lhsT) * 0.83) * (4.0 if rhs.dtype == F32 else 1.0) + 8.0
        self.P.add("tensor", lambda e: e.matmul(out, lhsT=lhsT, rhs=rhs, start=start, stop=stop), reads, writes, cost=c)

    def tr(self, out, in_, ident, reads, writes):
        self.P.add("tensor", lambda e: e.transpose(out=out, in_=in_, identity=ident), reads, writes, cost=90.0)

    def act(self, out, in_, func, reads, writes, scale=1.0, bias=None, accum=None):
        kw = {}
        if bias is not None:
            kw["bias"] = bias
        if accum is not None:
            kw["accum_out"] = accum
        c = _fs(in_) / 1.4 + (300.0 if accum is not None else 200.0)
        self.P.add("scalar", lambda e: e.activation(out=out, in_=in_, func=func, scale=scale, **kw), reads, writes, cost=c)

    def _vc(self, eng, ap):
        return _fs(ap) / 0.96 + 70.0 if eng == "vector" else _fs(ap) / 0.45 + 150.0

    def tt(self, out, in0, in1, op, reads, writes, eng="vector"):
        self.P.add(eng, lambda e: e.tensor_tensor(out=out, in0=in0, in1=in1, op=op), reads, writes, cost=self._vc(eng, out))

    def ts(self, out, in0, s1, op0, reads, writes, s2=None, op1=None, eng="vector", accum=None):
        kw = {}
        if op1 is not None:
            kw["op1"] = op1
        if accum is not None:
            kw["accum_out"] = accum
        self.P.add(eng, lambda e: e.tensor_scalar(out=out, in0=in0, scalar1=s1, scalar2=s2, op0=op0, **kw), reads, writes,
                   cost=self._vc(eng, out))

    def stt(self, out, in0, scalar, in1, op0, op1, reads, writes, eng="vector"):
        self.P.add(eng, lambda e: e.scalar_tensor_tensor(out=out, in0=in0, scalar=scalar, in1=in1, op0=op0, op1=op1), reads, writes,
                   cost=self._vc(eng, out))

    def cp(self, out, in_, reads, writes, eng="vector"):
        self.P.add(eng, lambda e: e.tensor_copy(out=out, in_=in_), reads, writes, cost=self._vc(eng, out))

    def recip(self, out, in_, reads, writes):
        self.P.add("vector", lambda e: e.reciprocal(out=out, in_=in_), reads, writes, cost=self._vc("vector", out))

    def memset(self, ap, val, writes, eng="gpsimd"):
        self.P.add(eng, lambda e: e.memset(ap, val), [], writes, cost=self._vc(eng, ap))

    def ld(self, tl, dst, src, eng="sync", extra_reads=(), **kw):
        c = 2000.0 + _nb(dst) / 0.12
        self.P.add(eng, lambda e: e.dma_start(out=dst, in_=src, **kw), list(extra_reads), [tl], dsem=tl.b.sem, cost=c)

    def st(self, dst, tl, src, eng="sync", **kw):
        c = 2000.0 + _nb(src) / 0.12
        self.P.add(eng, lambda e: e.dma_start(out=dst, in_=src, **kw), [tl], [], dsem=tl.b.sem, cost=c)

    def ldw(self, tl, dst, src):
        c = 3000.0 + _nb(src) / 0.12
        self.P.add("gpsimd", lambda e: e.dma_start(out=dst, in_=src, max_dma_last_dim=4096), [], [tl], dsem=tl.b.sem, cost=c)

    def ldw_all(self, tl, w2d, split=False, last_first=False):
        N = int(w2d.shape[1])
        cols = list(range(0, N, 1024))
        if last_first:
            cols = [cols[-1]] + cols[:-1]
        blocks = []
        for c0 in cols:
            w_ = min(1024, N - c0)
            if split:
                b_ = self.P.buf("wblk", "sw")
                self.live.append(b_)
                t_ = Tl(tl.ap, b_)
            else:
                t_ = tl
            src = w2d[:, c0:c0 + w_].rearrange("(c p) n -> p c n", p=128)
            dst = tl.ap[:, :, c0:c0 + w_]
            c = 3000.0 + _nb(src) / 0.3
            self.P.add("gpsimd", lambda e, dst=dst, src=src: e.dma_start(out=dst, in_=src, max_dma_last_dim=4096),
                       [], [t_], dsem=t_.b.sem, cost=c)
            blocks.append((c0, w_, t_))
        return blocks

    @staticmethod
    def wblk(blocks, col0):
        for (c0, w_, t_) in blocks:
            if c0 <= col0 < c0 + w_:
                return t_
        raise AssertionError("no weight block")

    def bank(self, i, n=1):
        return self.PS[:, 512 * i:512 * (i + n)]

    def bankb(self, i):
        return self.PS[:, 512 * i:512 * (i + 1)].bitcast(BF16)

    def build(self):
        nc = self.nc
        m = self
        m.h0 = m.din("h0", [T, D])
        m.consts = m.din("consts", [6, 128, 128])
        m.norm_mix = m.din("norm_mix_t", [4, 128, 8])
        m.norm_ffn = m.din("norm_ffn_t", [4, 128, 8])
        m.w_gate_up = m.din("w_gate_up", [4, D, 2 * DFF])
        m.w_down = m.din("w_down", [4, DFF, D])
        m.fox_w_in = m.din("fox_w_in", [2, D, 4112])
        m.fox_bf = m.din("fox_bf", [2, 16, 1])
        m.fox_qg = m.din("fox_qg", [2, 128, 1])
        m.fox_kg = m.din("fox_kg", [2, 128, 1])
        m.fox_w_out = m.din("fox_w_out", [2, D, D])
        m.gla_w_in = m.din("gla_w_in", [1, D, 3088])
        m.gla_w_alpha2 = m.din("gla_w_alpha2", [1, 16, 512])
        m.gla_ba = m.din("gla_ba", [1, 128, 4])
        m.gla_og = m.din("gla_og", [1, 64, D])
        m.gla_w_out = m.din("gla_w_out", [1, D, D])
        m.gdn_w_in = m.din("gdn_w_in", [1, D, 4112])
        m.gdn_cw = m.din("gdn_cw", [1, 128, 24, 4])
        m.gdn_alog = m.din("gdn_alog", [1, 16, 1])
        m.gdn_dtb = m.din("gdn_dtb", [1, 16, 1])
        m.gdn_og = m.din("gdn_og", [1, 64, D])
        m.gdn_w_out = m.din("gdn_w_out", [1, D, D])
        m.gconst = m.din("gconst", [16, 16, 128])
        m.gnegm = m.din("gnegm", [2, 64, 64])
        m.out = nc.dram_tensor("out", [4096, D], F32, kind="ExternalOutput").ap()
        if m.dbg in ("QA", "KA"):
            m.dbg_out = nc.dram_tensor("dbg", [16, 68, T], BF16, kind="ExternalOutput").ap()
        elif m.dbg:
            m.dbg_out = nc.dram_tensor("dbg", [T, D], F32, kind="ExternalOutput").ap()
        m.H = m.dscr("H", [T, D], F32)
        m.YT = m.dscr("YT", [NT, 128, 1024], BF16)
        m.OM = m.dscr("OM", [NT, 128, 1024], BF16)
        m.GS = m.dscr("GS", [NT, 128, 1024], BF16)
        m.VS = m.dscr("VS", [NT, 128, 1024], BF16)
        m.QA = m.dscr("QA", [16, 68, T], BF16)
        m.KA = m.dscr("KA", [16, 68, T], BF16)
        m.ACTT = m.dscr("ACTT", [NFC, 128, T], BF16)
        m.QG = m.dscr("QG", [8, 128, T], BF16)
        m.KG = m.dscr("KG", [8, 128, T], BF16)
        m.KT = m.dscr("KT", [NT, 128, 1024], BF16)
        m.QE = m.dscr("QE", [8, 128, T], BF16)
        m.TOKD = m.dscr("TOKD", [NT, 128, 32], F32)
        m.XAD = m.dscr("XAD", [66, 64, 1024], BF16)
        with contextlib.ExitStack() as st:
            m.SB = st.enter_context(nc.sbuf_tensor("arena", [128, ARENA_WORDS], F32))
            m.PS = st.enter_context(nc.psum_tensor("psum", [128, 4096], F32))
            m.pb = [m.P.buf("bank%d" % i) for i in range(8)]
            for b_ in m.pb:
                b_.excl = True
            m.live.extend(m.pb)
            m.dummy = m.tile(128, [2], F32, "dummy")
            cf = m.tile(128, [6, 128], F32, "cf", dma=True)
            m.ld(cf, cf.ap, m.consts.rearrange("c p f -> p c f"))
            cb = m.tile(128, [6, 128], BF16, "cb")
            m.cp(cb.ap, cf.ap, [cf], [cb])
            m.cf, m.cb = cf, cb
            m.identb = cb.ap[:, 0, :]
            m.Ub = cb.ap[:, 1, :]
            m.BDb = cb.ap[:, 5, :]
            m.gmix = m.tile(128, [4, 8], F32, "gmix", dma=True)
            m.ld(m.gmix, m.gmix.ap, m.norm_mix.rearrange("l p c -> p l c"))
            m.gffn = m.tile(128, [4, 8], F32, "gffn", dma=True)
            m.ld(m.gffn, m.gffn.ap, m.norm_ffn.rearrange("l p c -> p l c"))

            layers = list(range(m.nlayers)) if m.only is None else [m.only]
            m.first_layer = layers[0]
            m.prologue(layers[0])
            for l in layers:
                kind, j = l % 3, l // 3
                if m.stop == (l, "pre"):
                    break
                if kind == 0:
                    r = m.fox(j)
                elif kind == 1:
                    r = m.gla(j)
                else:
                    r = m.gdn(j)
                if m.stop == (l, "mix") or r == "stop":
                    break
                w_out = (m.fox_w_out, m.gla_w_out, m.gdn_w_out)[kind][j]
                m.c1(w_out, m.gffn.ap[:, l, :], (kind == 0), l)
                if m.stop == (l, "c1"):
                    break
                m.c2a(l)
                last = (l == layers[-1])
                m.c2b(l, None if last else m.gmix.ap[:, l + 1, :], last)
            if m.dbg:
                m.dump_dbg()
            m.barrier()
            m.P.add("sync", None, reads=[m.dummy])
            m.P.emit(m.window)
        return nc

    def norm_a(self, t, hT, w):
        m = self
        junk = w["junk"][t % 2]
        ss = w["ss"][t % 3]
        y = w["y"][t % 3]
        m.act(junk.ap, hT.ap, AF.Square, [hT], [junk, ss], accum=ss.ap[:, 0:1])
        m.act(ss.ap[:, 1:2], ss.ap[:, 0:1], AF.Sqrt, [ss], [ss], scale=1.0 / D, bias=EPS)
        m.recip(ss.ap[:, 2:3], ss.ap[:, 1:2], [ss], [ss])
        m.act(y.ap, hT.ap, AF.Copy, [hT, ss], [y], scale=ss.ap[:, 2:3])

    def norm_b(self, t, gain, w):
        m = self
        y = w["y"][t % 3]
        yt = w["yt"][t % 2]
        pbk = w["tbank"] + t % 2
        pT = m.bankb(pbk).rearrange("p (c k) -> p c k", c=8)
        for c in range(8):
            m.tr(pT[:, c, :], y.ap[:, c * 128:(c + 1) * 128], m.identb, [y, m.cb], [m.pb[pbk]])
        m.tt(yt.ap, pT, gain.unsqueeze(2).to_broadcast([128, 8, 128]), ALU.mult, [m.pb[pbk], m.gmix, m.gffn], [yt])
        m.st(m.YT[t], yt, yt.ap.rearrange("p c k -> p (c k)"))

    def norm_work(self, tbank):
        m = self
        return {
            "tbank": tbank,
            "junk": [m.tile(128, [D], BF16, "junk") for _ in range(2)],
            "ss": [m.tile(128, [4], F32, "ss") for _ in range(3)],
            "y": [m.tile(128, [D], BF16, "y") for _ in range(3)],
            "yt": [m.tile(128, [8, 128], BF16, "yt", dma=True) for _ in range(2)],
        }

    def prologue(self, l0=0):
        m = self
        mk = m.mark()
        w = m.norm_work(4)
        hs = [m.tile(128, [D], F32, "h", dma=True) for _ in range(3)]
        for i in range(NT + 1):
            if i < NT:
                t = i
                hT = hs[t % 3]
                m.ld(hT, hT.ap, m.h0[t * 128:(t + 1) * 128, :])
                m.norm_a(t, hT, w)
            if i - 1 >= 0:
                m.norm_b(i - 1, m.gmix.ap[:, l0, :], w)
        m.barrier()
        m.release(mk)

    def store_out(self, t, hT):
        m = self
        if t == 0:
            m.st(m.out[0:112, :], hT, hT.ap[16:128, :])
        elif t == 32:
            m.st(m.out[4080:4096, :], hT, hT.ap[0:16, :])
        else:
            m.st(m.out[t * 128 - 16:t * 128 + 112, :], hT, hT.ap)

    def c1(self, w_out, gain, gate, l):
        m = self
        mk = m.mark()
        W = m.tile(128, [8, D], BF16, "wout", dma="sw")
        m.ldw_all(W, w_out)
        m.Wgu = m.slot_tile([8, 2 * DFF], "wgu")
        m.ldw_all(m.Wgu, m.w_gate_up[l])
        w = m.norm_work(4)
        hs = [m.tile(128, [D], F32, "h", dma=True) for _ in range(4)]
        oms = [m.tile(128, [D], BF16, "om", dma=True) for _ in range(3)]
        gss = [m.tile(128, [D], BF16, "gs", dma=True) for _ in range(3)] if gate else None
        omT = [m.tile(128, [8, 128], BF16, "omT") for _ in range(3)]

        hsrc = m.h0 if l == m.first_layer else m.H

        def s1(t):
            hT = hs[t % 4]
            om = oms[t % 3]
            m.ld(hT, hT.ap, hsrc[t * 128:(t + 1) * 128, :])
            m.ld(om, om.ap, m.OM[t])
            if gate:
                gs = gss[t % 3]
                m.ld(gs, gs.ap, m.GS[t])
                m.tt(om.ap, om.ap, gs.ap, ALU.mult, [om, gs], [om])
            pbk = 6 + t % 2
            pT = m.bankb(pbk).rearrange("p (c k) -> p c k", c=8)
            for c in range(8):
                m.tr(pT[:, c, :], om.ap[:, c * 128:(c + 1) * 128], m.identb, [om, m.cb], [m.pb[pbk]])
            oT = omT[t % 3]
            m.act(oT.ap, pT, AF.Copy, [m.pb[pbk]], [oT])

        def s2(t):
            hT = hs[t % 4]
            oT = omT[t % 3]
            b0 = 2 * (t % 2)
            for nh in range(2):
                for kc in range(8):
                    m.mm(m.bank(b0 + nh), lhsT=oT.ap[:, kc, :], rhs=W.ap[:, kc, nh * 512:(nh + 1) * 512],
                         start=(kc == 0), stop=(kc == 7), reads=[oT, W], writes=[m.pb[b0 + nh]])
            m.tt(hT.ap, hT.ap, m.bank(b0, 2), ALU.add, [hT, m.pb[b0], m.pb[b0 + 1]], [hT])
            m.st(m.H[t * 128:(t + 1) * 128, :], hT, hT.ap)
            m.norm_a(t, hT, w)

        for i in range(NT + 2):
            if i < NT:
                s1(i)
            if 0 <= i - 1 < NT:
                s2(i - 1)
            if 0 <= i - 2 < NT:
                m.norm_b(i - 2, gain, w)
        m.barrier()
        m.release(mk)

    def c2a(self, l):
        m = self
        W = m.Wgu
        m.mkd = m.mark()
        m.Wd = m.tile(128, [NFC, D], BF16, "wd", dma="sw")
        m.ldw_all(m.Wd, m.w_down[l])
        mk = m.mark()
        ytgs = [m.tile(128, [4, 8, 128], BF16, "ytg", dma=True) for _ in range(2)]
        sgs = [m.tile(128, [512], F32, "sg") for _ in range(2)]
        acts = [m.tile(128, [512], BF16, "act", dma=True) for _ in range(3)]
        k = 0
        for gi, (t0, nt) in enumerate(GROUPS):
            GW = nt * 128
            ytg = ytgs[gi % 2]
            m.ld(ytg, ytg.ap[:, 0:nt].rearrange("p t c k -> p t (c k)"), m.YT[t0:t0 + nt].rearrange("t p f -> p t f"))
            for fc in range(NFC):
                bg = 2 * (k % 2)
                bu = bg + 1
                for which, bk in ((0, bg), (1, bu)):
                    col = which * DFF + fc * 128
                    o = m.bank(bk)[:, 0:GW].rearrange("p (t k) -> p t k", t=nt)
                    for kc in range(8):
                        m.mm(o, lhsT=W.ap[:, kc, col:col + 128], rhs=ytg.ap[:, 0:nt, kc, :],
                             start=(kc == 0), stop=(kc == 7), reads=[W, ytg], writes=[m.pb[bk]])
                sg = sgs[k % 2]
                a = acts[k % 3]
                m.act(sg.ap[:, 0:GW], m.bank(bg)[:, 0:GW], AF.Silu, [m.pb[bg]], [sg])
                m.tt(a.ap[:, 0:GW], sg.ap[:, 0:GW], m.bank(bu)[:, 0:GW], ALU.mult, [sg, m.pb[bu]], [a])
                m.st(m.ACTT[fc, :, t0 * 128:t0 * 128 + GW], a, a.ap[:, 0:GW])
                k += 1
        m.barrier()
        m.release(mk)
        m.slot_free(m.Wgu)

    def c2b(self, l, gain, last):
        m = self
        mk = m.mkd
        W = m.Wd
        w = None if last else m.norm_work(4)
        hs = [m.tile(128, [D], F32, "h", dma=True) for _ in range(3)]
        ags = [m.tile(128, [NFC, 512], BF16, "ag", dma=True) for _ in range(2)]
        tiles = []
        for gi, (t0, nt) in enumerate(GROUPS):
            for ti in range(nt):
                tiles.append((gi, t0, nt, ti))

        def s2(k):
            gi, t0, nt, ti = tiles[k]
            GW = nt * 128
            ag = ags[gi % 2]
            if ti == 0:
                m.ld(ag, ag.ap[:, :, 0:GW], m.ACTT[:, :, t0 * 128:t0 * 128 + GW].rearrange("f p t -> p f t"))
            t = t0 + ti
            hT = hs[k % 3]
            m.ld(hT, hT.ap, m.H[t * 128:(t + 1) * 128, :])
            b0 = 2 * (k % 2)
            for nh in range(2):
                for fc in range(NFC):
                    m.mm(m.bank(b0 + nh), lhsT=ag.ap[:, fc, ti * 128:(ti + 1) * 128], rhs=W.ap[:, fc, nh * 512:(nh + 1) * 512],
                         start=(fc == 0), stop=(fc == NFC - 1), reads=[ag, W], writes=[m.pb[b0 + nh]])
            m.tt(hT.ap, hT.ap, m.bank(b0, 2), ALU.add, [hT, m.pb[b0], m.pb[b0 + 1]], [hT])
            if last:
                m.store_out(t, hT)
                if m.dbg:
                    m.st(m.H[t * 128:(t + 1) * 128, :], hT, hT.ap)
            else:
                m.st(m.H[t * 128:(t + 1) * 128, :], hT, hT.ap)
                m.norm_a(t, hT, w)

        for i in range(NT + 1):
            if i < NT:
                s2(i)
            if not last and 0 <= i - 1 < NT:
                m.norm_b(i - 1, gain, w)
        m.barrier()
        m.release(mk)

    def dump_dbg(self):
        m = self
        mk = m.mark()
        if m.dbg in ("QA", "KA"):
            tt_ = [m.tile(68, [T], BF16, "dq", dma=True) for _ in range(2)]
            for h in range(16):
                m.ld(tt_[h % 2], tt_[h % 2].ap, getattr(m, m.dbg)[h])
                m.st(m.dbg_out[h], tt_[h % 2], tt_[h % 2].ap)
            m.barrier()
            m.release(mk)
            return
        src = {"H": m.H}.get(m.dbg, None)
        hs = [m.tile(128, [D], F32, "h", dma=True) for _ in range(3)]
        for t in range(NT):
            hT = hs[t % 3]
            if src is not None:
                m.ld(hT, hT.ap, src[t * 128:(t + 1) * 128, :])
            else:
                bt = m.tile(128, [D], BF16, "bt", dma=True) if t == 0 else bt
                m.ld(bt, bt.ap, getattr(m, m.dbg)[t])
                m.cp(hT.ap, bt.ap, [bt], [hT])
            m.st(m.dbg_out[t * 128:(t + 1) * 128, :], hT, hT.ap)
        m.barrier()
        m.release(mk)

    def fox(self, j):
        m = self
        mk0 = m.mark()
        lf = m.tile(16, [T], F32, "lf")
        mk = m.mark()
        Wb = []
        for (c0_, w_) in ((0, 1024), (1024, 1024), (4096, 16), (2048, 1024), (3072, 1024)):
            tl_ = m.tile(128, [8, w_], BF16, "win", dma="sw")
            m.ldw_all(tl_, m.fox_w_in[j, :, c0_:c0_ + w_])
            Wb.append((c0_, w_, tl_))

        def wsl(kc, col0, width):
            for (c0_, w_, tl_) in Wb:
                if c0_ <= col0 and col0 + width <= c0_ + w_:
                    return tl_, tl_.ap[:, kc, col0 - c0_:col0 - c0_ + width]
            raise AssertionError("no weight block")
        gq = m.tile(128, [2], F32, "gq", dma=True)
        gk = m.tile(128, [1], F32, "gk", dma=True)
        nbf = m.tile(16, [1], F32, "nbf", dma=True)
        m.ld(gq, gq.ap[:, 0:1], m.fox_qg[j])
        m.ld(gk, gk.ap, m.fox_kg[j])
        m.ld(nbf, nbf.ap, m.fox_bf[j])
        m.ts(nbf.ap, nbf.ap, -1.0, ALU.mult, [nbf], [nbf])
        m.ts(gq.ap[:, 1:2], gq.ap[:, 0:1], 0.125, ALU.mult, [gq], [gq])
        ytgs = [m.tile(128, [4, 8, 128], BF16, "ytg", dma=True) for _ in range(2)]
        sqs = [m.tile(128, [512], BF16, "sq") for _ in range(3)]
        sds = [m.tile(128, [512], F32, "sd") for _ in range(3)]
        qhs = [m.tile(128, [512], BF16, "qh", dma=True) for _ in range(3)]
        vos = [m.tile(128, [D], BF16, "vo", dma=True) for _ in range(2)]
        gos = [m.tile(128, [D], BF16, "go", dma=True) for _ in range(2)]
        et = m.tile(16, [512], F32, "et")
        k = 0
        kt = 0
        for gi, (t0, nt) in enumerate(GROUPS):
            GW = nt * 128
            tok0 = t0 * 128
            ytg = ytgs[gi % 2]
            m.ld(ytg, ytg.ap[:, 0:nt].rearrange("p t c k -> p t (c k)"), m.YT[t0:t0 + nt].rearrange("t p f -> p t f"))
            for c in range(17):
                bk = k % 4
                M = 128 if c < 16 else 16
                cw0 = c * 128 if c < 16 else 4096
                o = m.bank(bk)[0:M, 0:GW].rearrange("p (t k) -> p t k", t=nt)
                for kc in range(8):
                    wt_, wap_ = wsl(kc, cw0, M)
                    m.mm(o, lhsT=wap_, rhs=ytg.ap[:, 0:nt, kc, :],
                         start=(kc == 0), stop=(kc == 7), reads=[wt_, ytg], writes=[m.pb[bk]])
                if c == 16:
                    m.act(et.ap[:, 0:GW], m.bank(bk)[0:16, 0:GW], AF.Exp, [m.pb[bk], nbf], [et], scale=-1.0, bias=nbf.ap[:, 0:1])
                    m.act(lf.ap[:, tok0:tok0 + GW], et.ap[:, 0:GW], AF.Ln, [et], [lf], bias=1.0)
                else:
                    sq = sqs[k % 3]
                    sd = sds[k % 3]
                    qh = qhs[k % 3]
                    bs = 4 + k % 2
                    m.act(sq.ap[:, 0:GW], m.bank(bk)[:, 0:GW], AF.Square, [m.pb[bk]], [sq])
                    m.mm(m.bank(bs)[:, 0:GW], lhsT=m.BDb, rhs=sq.ap[:, 0:GW], start=True, stop=True, reads=[sq, m.cb], writes=[m.pb[bs]])
                    m.act(sd.ap[:, 0:GW], m.bank(bs)[:, 0:GW], AF.Sqrt, [m.pb[bs]], [sd], scale=1.0 / 64, bias=EPS)
                    m.recip(sd.ap[:, 0:GW], sd.ap[:, 0:GW], [sd], [sd])
                    g = gq.ap[:, 1:2] if c < 8 else gk.ap[:, 0:1]
                    m.stt(qh.ap[:, 0:GW], m.bank(bk)[:, 0:GW], g, sd.ap[:, 0:GW], ALU.mult, ALU.mult, [m.pb[bk], sd, gq, gk], [qh])
                    dst = m.QA if c < 8 else m.KA
                    hd = 2 * (c % 8)
                    m.st(dst[hd, 0:64, tok0:tok0 + GW], qh, qh.ap[0:64, 0:GW])
                    m.st(dst[hd + 1, 0:64, tok0:tok0 + GW], qh, qh.ap[64:128, 0:GW])
                k += 1
            for ti in range(nt):
                t = t0 + ti
                for which in range(2):
                    col0 = 2048 + which * 1024
                    b0 = 6
                    for nh in range(2):
                        for kc in range(8):
                            wt_, wap_ = wsl(kc, col0 + nh * 512, 512)
                            m.mm(m.bank(b0 + nh), lhsT=ytg.ap[:, ti, kc, :], rhs=wap_,
                                 start=(kc == 0), stop=(kc == 7), reads=[ytg, wt_], writes=[m.pb[b0 + nh]])
                    if which == 0:
                        vo = vos[kt % 2]
                        m.cp(vo.ap, m.bank(b0, 2), [m.pb[b0], m.pb[b0 + 1]], [vo])
                        m.st(m.VS[t], vo, vo.ap)
                    else:
                        go = gos[kt % 2]
                        m.act(go.ap, m.bank(b0, 2), AF.Sigmoid, [m.pb[b0], m.pb[b0 + 1]], [go])
                        m.st(m.GS[t], go, go.ap)
                kt += 1
        m.barrier()
        m.release(mk)
        mk = m.mark()
        ones = m.tile(16, [T], F32, "ones")
        C = m.tile(16, [T], F32, "C")
        m.memset(ones.ap, 1.0, [ones])
        m.P.add("vector", lambda e: e.tensor_tensor_scan(out=C.ap, data0=ones.ap, data1=lf.ap, initial=0.0,
                                                         op0=ALU.mult, op1=ALU.subtract), [ones, lf], [C])
        chi = m.tile(16, [T], BF16, "chi", dma=True)
        clo = m.tile(16, [T], BF16, "clo", dma=True)
        nchi = m.tile(16, [T], BF16, "nchi", dma=True)
        nclo = m.tile(16, [T], BF16, "nclo", dma=True)
        oneb = m.tile(16, [T], BF16, "oneb", dma=True)
        m.memset(oneb.ap, 1.0, [oneb])
        m.cp(chi.ap, C.ap, [C], [chi])
        m.tt(clo.ap, C.ap, chi.ap, ALU.subtract, [C, chi], [clo])
        m.ts(nchi.ap, chi.ap, -1.0, ALU.mult, [chi], [nchi])
        m.ts(nclo.ap, clo.ap, -1.0, ALU.mult, [clo], [nclo])
        for row, (qsrc, ksrc) in enumerate(((chi, oneb), (clo, oneb), (oneb, nchi), (oneb, nclo))):
            m.st(m.QA[:, 64 + row, :], qsrc, qsrc.ap)
            m.st(m.KA[:, 64 + row, :], ksrc, ksrc.ap)
        m.barrier()
        m.release(mk)
        m.release(mk0)
        mk = m.mark()
        OMS = m.tile(128, [NT, D], BF16, "oms", dma=True)
        qa = [m.tile(68, [T], BF16, "qa", dma=True) for _ in range(2)]
        ka = [m.tile(68, [T], BF16, "ka", dma=True) for _ in range(2)]
        v1 = [m.tile(128, [NT, 65], BF16, "v1", dma=True) for _ in range(2)]
        pts = [m.tile(128, [2, 512], BF16, "pt") for _ in range(3)]
        rec = m.tile(128, [4], F32, "rec")
        for s in range(2):
            m.memset(v1[s].ap[:, :, 64:65], 1.0, [v1[s]], eng="vector")

        def loads(h):
            s = h % 2
            m.ld(qa[s], qa[s].ap, m.QA[h])
            m.ld(ka[s], ka[s].ap, m.KA[h])
            for a in range(3):
                m.ld(v1[s], v1[s].ap[:, a * 11:(a + 1) * 11, 0:64],
                     m.VS[a * 11:(a + 1) * 11, :, h * 64:(h + 1) * 64].rearrange("t p d -> p t d"))

        loads(0)
        osb = [m.tile(65, [512], F32, "osb") for _ in range(2)]
        identf = m.cf.ap[:, 0, :]
        LA = 2
        cnt = 0
        gcnt = 0
        for h in range(16):
            s = h % 2
            if h + 1 < 16:
                loads(h + 1)
            steps = []
            for gi, (t0, nt) in enumerate(GROUPS):
                for J in range(0, t0, 2):
                    steps.append((gi, t0, nt, (J, J + 1)))
                for J in range(t0, t0 + nt):
                    steps.append((gi, t0, nt, (J,)))

            def emit_s(k):
                gi, t0, nt, Js = steps[k]
                Wq = nt * 128
                q0 = t0 * 128
                g2 = (cnt + k) % 2
                p = pts[(cnt + k) % 3]
                sb0 = 4 + 2 * g2
                for i, J in enumerate(Js):
                    c0 = max(0, J - t0) * 128
                    m.mm(m.bank(sb0 + i)[:, c0:Wq], lhsT=ka[s].ap[:, J * 128:(J + 1) * 128], rhs=qa[s].ap[:, q0 + c0:q0 + Wq],
                         start=True, stop=True, reads=[ka[s], qa[s]], writes=[m.pb[sb0 + i]])
                if len(Js) == 2:
                    src = m.bank(sb0, 2).rearrange("p (b w) -> p b w", b=2)[:, :, 0:Wq]
                    m.act(p.ap[:, :, 0:Wq], src, AF.Exp, [m.pb[sb0], m.pb[sb0 + 1]], [p])
                else:
                    J = Js[0]
                    c0 = max(0, J - t0) * 128
                    m.act(p.ap[:, 0, c0:Wq], m.bank(sb0)[:, c0:Wq], AF.Exp, [m.pb[sb0]], [p])
                    m.tt(p.ap[:, 0, c0:c0 + 128], p.ap[:, 0, c0:c0 + 128], m.Ub, ALU.mult, [p, m.cb], [p], eng="gpsimd")

            def emit_pv(k):
                gi, t0, nt, Js = steps[k]
                Wq = nt * 128
                p = pts[(cnt + k) % 3]
                ab = (gcnt + gi) % 2
                for i, J in enumerate(Js):
                    c0 = max(0, J - t0) * 128
                    last = (J == t0 + nt - 1)
                    m.mm(m.bank(ab)[0:65, c0:Wq], lhsT=v1[s].ap[:, J, :], rhs=p.ap[:, i, c0:Wq],
                         start=(J == 0), stop=last, reads=[p, v1[s]], writes=[m.pb[ab]])
                    if last:
                        ob = osb[(gcnt + gi) % 2]
                        m.cp(ob.ap[:, 0:Wq], m.bank(ab)[0:65, 0:Wq], [m.pb[ab]], [ob])
                        pt2 = m.bank(2)[:, 0:nt * 65].rearrange("p (t f) -> p t f", t=nt)
                        for ii in range(nt):
                            m.tr(pt2[:, ii, :], ob.ap[:, ii * 128:(ii + 1) * 128], identf[0:65, 0:65], [ob, m.cf], [m.pb[2]])
                        m.recip(rec.ap[:, 0:nt], pt2[:, :, 64], [m.pb[2]], [rec])
                        m.tt(OMS.ap[:, t0:t0 + nt, h * 64:(h + 1) * 64], pt2[:, :, 0:64],
                             rec.ap[:, 0:nt].unsqueeze(2).to_broadcast([128, nt, 64]), ALU.mult, [m.pb[2], rec], [OMS])

            n = len(steps)
            for k in range(n + LA):
                if k < n:
                    emit_s(k)
                if k - LA >= 0:
                    emit_pv(k - LA)
            cnt += n
            gcnt += len(GROUPS)
        for a in range(3):
            m.st(m.OM[a * 11:(a + 1) * 11].rearrange("t p d -> p t d"), OMS, OMS.ap[:, a * 11:(a + 1) * 11, :])
        m.barrier()
        m.release(mk)


    def gla(self, j):
        m = self
        mk0 = m.mark()
        ELAST = m.tile(128, [4, 66], F32, "elast")
        mk = m.mark()
        W = m.tile(128, [8, 3088], BF16, "win", dma="sw")
        WB = m.ldw_all(W, m.gla_w_in[j], split=True, last_first=True)
        Wa = m.tile(16, [512], BF16, "wa", dma="sw")
        m.ldw(Wa, Wa.ap, m.gla_w_alpha2[j])
        nba = m.tile(128, [4], F32, "nba", dma=True)
        m.ld(nba, nba.ap, m.gla_ba[j])
        m.ts(nba.ap, nba.ap, -1.0, ALU.mult, [nba], [nba])
        seg = m.tile(128, [512], F32, "seg")
        m.memset(seg.ap, 1.0, [seg])
        m.memset(seg.ap.rearrange("p (c s) -> p c s", s=64)[:, :, 0:1], 0.0, [seg])
        ytgs = [m.tile(128, [4, 8, 128], BF16, "ytg", dma=True) for _ in range(2)]
        alrT = m.tile(16, [512], BF16, "alrT")
        sets = [{n: m.tile(128, [512], F32, n) for n in ("e1", "cs", "eb", "enb", "edec")} for _ in range(2)]
        ncls = [m.tile(128, [8], F32, "ncl") for _ in range(2)]
        qgs = [m.tile(128, [512], BF16, "qg", dma=True) for _ in range(2)]
        kgs = [m.tile(128, [512], BF16, "kg", dma=True) for _ in range(2)]
        kds = [m.tile(128, [512], BF16, "kd") for _ in range(2)]
        kdts = [m.tile(128, [4, 128], BF16, "kdt", dma=True) for _ in range(2)]
        vos = [m.tile(128, [D], BF16, "vo", dma=True) for _ in range(2)]
        gos = [m.tile(128, [D], BF16, "go", dma=True) for _ in range(2)]
        k = 0
        kh = 0
        kt = 0
        for gi, (t0, nt) in enumerate(GROUPS):
            GW = nt * 128
            tok0 = t0 * 128
            nch = 2 * nt
            ch0 = 2 * t0
            ytg = ytgs[gi % 2]
            m.ld(ytg, ytg.ap[:, 0:nt].rearrange("p t c k -> p t (c k)"), m.YT[t0:t0 + nt].rearrange("t p f -> p t f"))

            def fjob(col0, M):
                nonlocal k
                bk = k % 3
                k += 1
                o = m.bank(bk)[0:M, 0:GW].rearrange("p (t k) -> p t k", t=nt)
                for kc in range(8):
                    m.mm(o, lhsT=W.ap[:, kc, col0:col0 + M], rhs=ytg.ap[:, 0:nt, kc, :],
                         start=(kc == 0), stop=(kc == 7), reads=[m.wblk(WB, col0), ytg], writes=[m.pb[bk]])
                return bk

            bk = fjob(3072, 16)
            m.cp(alrT.ap[:, 0:GW], m.bank(bk)[0:16, 0:GW], [m.pb[bk]], [alrT])
            for h in range(4):
                S_ = sets[kh % 2]
                e1, cs, eb, enb, edec = S_["e1"], S_["cs"], S_["eb"], S_["enb"], S_["edec"]
                ncl = ncls[kh % 2]
                qg, kg, kd, kdt = qgs[kh % 2], kgs[kh % 2], kds[kh % 2], kdts[kh % 2]
                bg = k % 3
                k += 1
                m.mm(m.bank(bg)[:, 0:GW], lhsT=Wa.ap[:, h * 128:(h + 1) * 128], rhs=alrT.ap[:, 0:GW], start=True, stop=True,
                     reads=[Wa, alrT], writes=[m.pb[bg]])
                m.act(e1.ap[:, 0:GW], m.bank(bg)[:, 0:GW], AF.Exp, [m.pb[bg], nba], [e1], scale=-1.0, bias=nba.ap[:, h:h + 1])
                m.act(e1.ap[:, 0:GW], e1.ap[:, 0:GW], AF.Ln, [e1], [e1], bias=1.0)
                m.P.add("vector", lambda e, cs=cs, e1=e1, GW=GW: e.tensor_tensor_scan(
                    out=cs.ap[:, 0:GW], data0=seg.ap[:, 0:GW], data1=e1.ap[:, 0:GW], initial=0.0, op0=ALU.mult, op1=ALU.add),
                    [seg, e1], [cs])
                m.act(eb.ap[:, 0:GW], cs.ap[:, 0:GW], AF.Exp, [cs], [eb], scale=-1.0 / 16)
                m.act(enb.ap[:, 0:GW], cs.ap[:, 0:GW], AF.Exp, [cs], [enb], scale=1.0 / 16)
                m.ts(ncl.ap[:, 0:nch], cs.ap[:, 0:GW].rearrange("p (c s) -> p c s", s=64)[:, :, 63], -1.0 / 16, ALU.mult, [cs], [ncl])
                m.act(ELAST.ap[:, h, ch0:ch0 + nch], ncl.ap[:, 0:nch], AF.Exp, [ncl], [ELAST])
                m.tt(edec.ap[:, 0:GW].rearrange("p (c s) -> p c s", s=64), enb.ap[:, 0:GW].rearrange("p (c s) -> p c s", s=64),
                     ELAST.ap[:, h, ch0:ch0 + nch].unsqueeze(2).to_broadcast([128, nch, 64]), ALU.mult, [enb, ELAST], [edec])
                bq = fjob(h * 128, 128)
                m.stt(qg.ap[:, 0:GW], m.bank(bq)[:, 0:GW], 128 ** -0.5, eb.ap[:, 0:GW], ALU.mult, ALU.mult, [m.pb[bq], eb], [qg])
                m.st(m.QG[h, :, tok0:tok0 + GW], qg, qg.ap[:, 0:GW])
                bkk = fjob(512 + h * 128, 128)
                m.tt(kg.ap[:, 0:GW], m.bank(bkk)[:, 0:GW], enb.ap[:, 0:GW], ALU.mult, [m.pb[bkk], enb], [kg])
                m.st(m.KG[h, :, tok0:tok0 + GW], kg, kg.ap[:, 0:GW])
                m.tt(kd.ap[:, 0:GW], m.bank(bkk)[:, 0:GW], edec.ap[:, 0:GW], ALU.mult, [m.pb[bkk], edec], [kd])
                bt = 3
                pT = m.bankb(bt)[:, 0:GW].rearrange("p (t d) -> p t d", t=nt)
                for ti in range(nt):
                    m.tr(pT[:, ti, :], kd.ap[:, ti * 128:(ti + 1) * 128], m.identb, [kd, m.cb], [m.pb[bt]])
                m.act(kdt.ap[:, 0:nt, :], pT, AF.Copy, [m.pb[bt]], [kdt])
                m.st(m.KT[t0:t0 + nt, :, h * 128:(h + 1) * 128].rearrange("t p d -> p t d"), kdt, kdt.ap[:, 0:nt, :])
                kh += 1
            for ti in range(nt):
                t = t0 + ti
                for which in range(2):
                    col0 = 1024 + which * 1024
                    b0 = 4 + 2 * which
                    for nh in range(2):
                        for kc in range(8):
                            m.mm(m.bank(b0 + nh), lhsT=ytg.ap[:, ti, kc, :], rhs=W.ap[:, kc, col0 + nh * 512:col0 + (nh + 1) * 512],
                                 start=(kc == 0), stop=(kc == 7), reads=[ytg, m.wblk(WB, col0 + nh * 512)], writes=[m.pb[b0 + nh]])
                    if which == 0:
                        vo = vos[kt % 2]
                        m.cp(vo.ap, m.bank(b0, 2), [m.pb[b0], m.pb[b0 + 1]], [vo])
                        m.st(m.VS[t], vo, vo.ap)
                    else:
                        go = gos[kt % 2]
                        m.act(go.ap, m.bank(b0, 2), AF.Silu, [m.pb[b0], m.pb[b0 + 1]], [go])
                        m.st(m.GS[t], go, go.ap)
                kt += 1
        m.barrier()
        m.release(mk)
        mk = m.mark()
        og = m.tile(64, [D], F32, "og", dma=True)
        m.ld(og, og.ap, m.gla_og[j])
        S = m.tile(128, [4, 256], F32, "S")
        m.memset(S.ap, 0.0, [S])
        Sbf = [m.tile(128, [4, 256], BF16, "Sbf") for _ in range(3)]
        m.memset(Sbf[2].ap, 0.0, [Sbf[2]])
        gs_ = [{"qg": m.tile(128, [4, 512], BF16, "qg", dma=True), "kg": m.tile(128, [4, 512], BF16, "kg", dma=True),
                "kdt": m.tile(64, [8, 512], BF16, "kdt", dma=True), "vs": m.tile(64, [8, D], BF16, "vs", dma=True),
                "rs": m.tile(64, [8, D], BF16, "rs", dma=True)} for _ in range(2)]
        Ats = [m.tile(64, [4, 64], BF16, "At") for _ in range(2)]
        junk = m.tile(64, [256], BF16, "junk")
        sss = [m.tile(64, [8], F32, "ss") for _ in range(2)]
        ons = [m.tile(64, [D], F32, "on") for _ in range(2)]
        oms = [m.tile(64, [D], BF16, "om", dma=True) for _ in range(2)]
        KTf = m.KT.rearrange("t p f -> (t p) f")
        VSf = m.VS.rearrange("t p f -> (t p) f")
        GSf = m.GS.rearrange("t p f -> (t p) f")
        OMf = m.OM.rearrange("t p f -> (t p) f")

        def loads(gi):
            t0, nt = GROUPS[gi]
            g = gs_[gi % 2]
            GW = nt * 128
            tok0 = t0 * 128
            nch = 2 * nt
            m.ld(g["qg"], g["qg"].ap[:, :, 0:GW], m.QG[0:4, :, tok0:tok0 + GW].rearrange("h d t -> d h t"))
            m.ld(g["kg"], g["kg"].ap[:, :, 0:GW], m.KG[0:4, :, tok0:tok0 + GW].rearrange("h d t -> d h t"))
            m.ld(g["kdt"], g["kdt"].ap[:, 0:nch, :], KTf[tok0:tok0 + GW, 0:512].rearrange("(c s) f -> s c f", s=64))
            m.ld(g["vs"], g["vs"].ap[:, 0:nch, :], VSf[tok0:tok0 + GW, :].rearrange("(c s) f -> s c f", s=64))
            m.ld(g["rs"], g["rs"].ap[:, 0:nch, :], GSf[tok0:tok0 + GW, :].rearrange("(c s) f -> s c f", s=64))

        loads(0)
        c = 0
        U64 = m.Ub[0:64, 0:64].unsqueeze(1).to_broadcast([64, 4, 64])
        for gi, (t0, nt) in enumerate(GROUPS):
            if gi + 1 < len(GROUPS):
                loads(gi + 1)
            g = gs_[gi % 2]
            qg, kg, kdt, vs, rs = g["qg"], g["kg"], g["kdt"], g["vs"], g["rs"]
            tok0 = t0 * 128
            for cc in range(2 * nt):
                cs_ = slice(cc * 64, (cc + 1) * 64)
                ab = (c % 2) * 256
                At = Ats[c % 2]
                for h in range(4):
                    m.mm(m.bank(0)[0:64, ab + h * 64:ab + (h + 1) * 64], lhsT=kg.ap[:, h, cs_], rhs=qg.ap[:, h, cs_],
                         start=True, stop=True, reads=[kg, qg], writes=[m.pb[0]])
                m.tt(At.ap, m.bank(0)[0:64, ab:ab + 256].rearrange("p (h t) -> p h t", h=4), U64, ALU.mult, [m.pb[0], m.cb], [At])
                for h in range(4):
                    m.mm(m.bank(1 + h // 2)[:, (h % 2) * 256:(h % 2 + 1) * 256], lhsT=kdt.ap[:, cc, h * 128:(h + 1) * 128],
                         rhs=vs.ap[:, cc, h * 256:(h + 1) * 256], start=True, stop=True, reads=[kdt, vs], writes=[m.pb[1 + h // 2]])
                for h in range(4):
                    m.stt(S.ap[:, h, :], S.ap[:, h, :], ELAST.ap[:, h, c:c + 1], m.bank(1 + h // 2)[:, (h % 2) * 256:(h % 2 + 1) * 256],
                          ALU.mult, ALU.add, [S, ELAST, m.pb[1 + h // 2]], [S])
                m.act(Sbf[c % 3].ap, S.ap, AF.Copy, [S], [Sbf[c % 3]])
                ob = 3 + 2 * (c % 2)
                Sp = Sbf[(c - 1) % 3]
                for h in range(4):
                    oo = m.bank(ob + h // 2)[0:64, (h % 2) * 256:(h % 2 + 1) * 256]
                    m.mm(oo, lhsT=At.ap[:, h, :], rhs=vs.ap[:, cc, h * 256:(h + 1) * 256], start=True, stop=False,
                         reads=[At, vs], writes=[m.pb[ob + h // 2]])
                    m.mm(oo, lhsT=qg.ap[:, h, cs_], rhs=Sp.ap[:, h, :], start=False, stop=True,
                         reads=[qg, Sp], writes=[m.pb[ob + h // 2]])
                ss = sss[c % 2]
                on = ons[c % 2]
                om = oms[c % 2]
                for h in range(4):
                    m.act(junk.ap, m.bank(ob + h // 2)[0:64, (h % 2) * 256:(h % 2 + 1) * 256], AF.Square,
                          [m.pb[ob + h // 2]], [junk, ss], accum=ss.ap[:, h:h + 1])
                m.act(ss.ap[:, 4:8], ss.ap[:, 0:4], AF.Sqrt, [ss], [ss], scale=1.0 / 256, bias=EPS)
                m.recip(ss.ap[:, 4:8], ss.ap[:, 4:8], [ss], [ss])
                m.tt(on.ap.rearrange("p (h v) -> p h v", h=4), m.bank(ob, 2)[0:64, :].rearrange("p (h v) -> p h v", h=4),
                     ss.ap[:, 4:8].unsqueeze(2).to_broadcast([64, 4, 256]), ALU.mult, [m.pb[ob], m.pb[ob + 1], ss], [on])
                m.tt(on.ap, on.ap, og.ap, ALU.mult, [on, og], [on], eng="gpsimd")
                m.tt(om.ap, on.ap, rs.ap[:, cc, :], ALU.mult, [on, rs], [om], eng="gpsimd")
                m.st(OMf[tok0 + cc * 64:tok0 + (cc + 1) * 64, :], om, om.ap)
                c += 1
        m.barrier()
        m.release(mk)
        m.release(mk0)


    def gdn(self, j):
        m = self
        mk0 = m.mark()
        B = m.tile(16, [T], F32, "B")
        NB = m.tile(16, [T], F32, "NB")
        ELB = m.tile(128, [66, 8], F32, "ELB")
        sel = m.tile(16, [16, 128], F32, "sel", dma=True)
        m.ld(sel, sel.ap, m.gconst)
        identf = m.cf.ap[:, 0, :]
        mk = m.mark()
        W = m.tile(128, [8, 4112], BF16, "win", dma="sw")
        WB = m.ldw_all(W, m.gdn_w_in[j], split=True, last_first=True)
        cw = m.tile(128, [24, 4], F32, "cw", dma=True)
        m.ld(cw, cw.ap, m.gdn_cw[j])
        carry = m.tile(128, [24, 4], BF16, "carry")
        m.memset(carry.ap, 0.0, [carry])
        al = m.tile(16, [2], F32, "al", dma=True)
        dtb = m.tile(16, [1], F32, "dtb", dma=True)
        m.ld(al, al.ap[:, 0:1], m.gdn_alog[j])
        m.ld(dtb, dtb.ap, m.gdn_dtb[j])
        m.act(al.ap[:, 1:2], al.ap[:, 0:1], AF.Exp, [al], [al])
        m.ts(al.ap[:, 1:2], al.ap[:, 1:2], -1.0, ALU.mult, [al], [al])
        seg = m.tile(16, [512], F32, "seg")
        m.memset(seg.ap, 1.0, [seg])
        m.memset(seg.ap.rearrange("p (c s) -> p c s", s=64)[:, :, 0:1], 0.0, [seg])
        onesb = m.tile(128, [128], BF16, "onesb")
        m.memset(onesb.ap, 1.0, [onesb])
        ytgs = [m.tile(128, [4, 8, 128], BF16, "ytg", dma=True) for _ in range(2)]
        xps = [m.tile(128, [516], BF16, "xp") for _ in range(3)]
        dwts = [m.tile(128, [4, 128], BF16, "dwt") for _ in range(2)]
        r3s = [m.tile(128, [512], F32, "r3") for _ in range(1)]
        sils = [m.tile(128, [512], F32, "sil") for _ in range(3)]
        sqs = [m.tile(128, [512], BF16, "sq") for _ in range(3)]
        sds = [m.tile(128, [512], F32, "sd") for _ in range(2)]
        outb = [m.tile(128, [512], BF16, "ob", dma=True) for _ in range(4)]
        ktts = [m.tile(128, [4, 128], BF16, "ktt", dma=True) for _ in range(2)]
        gos = [m.tile(128, [D], BF16, "go", dma=True) for _ in range(2)]
        e1 = m.tile(16, [512], F32, "e1")
        cs = m.tile(16, [512], F32, "cs")
        EB = m.tile(16, [512], F32, "EB")
        BETA = m.tile(16, [512], F32, "BETA")
        EDEC = cs
        tmp = e1
        bl = m.tile(16, [8], F32, "bl")
        EL16 = m.tile(16, [66], F32, "EL16")
        elx = m.tile(16, [8, 128], F32, "elx")
        toks = [m.tile(128, [4, 32], F32, "tok", dma=True) for _ in range(2)]
        k = 0
        kj = 0
        ko = 0
        kx = 0
        kt = 0
        for gi, (t0, nt) in enumerate(GROUPS):
            GW = nt * 128
            tok0 = t0 * 128
            nch = 2 * nt
            ch0 = 2 * t0
            ytg = ytgs[gi % 2]
            m.ld(ytg, ytg.ap[:, 0:nt].rearrange("p t c k -> p t (c k)"), m.YT[t0:t0 + nt].rearrange("t p f -> p t f"))

            def fjob(col0, M):
                nonlocal k
                bk = k % 2
                k += 1
                o = m.bank(bk)[0:M, 0:GW].rearrange("p (t k) -> p t k", t=nt)
                for kc in range(8):
                    m.mm(o, lhsT=W.ap[:, kc, col0:col0 + M], rhs=ytg.ap[:, 0:nt, kc, :],
                         start=(kc == 0), stop=(kc == 7), reads=[m.wblk(WB, col0), ytg], writes=[m.pb[bk]])
                return bk

            def tok_major(src, dstD, h):
                nonlocal kx
                pT = m.bankb(4)[:, 0:GW].rearrange("p (t d) -> p t d", t=nt)
                for ti in range(nt):
                    m.tr(pT[:, ti, :], src.ap[:, ti * 128:(ti + 1) * 128], m.identb, [src, m.cb], [m.pb[4]])
                ktt = ktts[kx % 2]
                kx += 1
                m.act(ktt.ap[:, 0:nt, :], pT, AF.Copy, [m.pb[4]], [ktt])
                m.st(dstD[t0:t0 + nt, :, h * 128:(h + 1) * 128].rearrange("t p d -> p t d"), ktt, ktt.ap[:, 0:nt, :])

            bk = fjob(4096, 16)
            Bg = B.ap[:, tok0:tok0 + GW]
            m.act(e1.ap[:, 0:GW], m.bank(bk)[0:16, 0:GW], AF.Exp, [m.pb[bk], dtb], [e1], bias=dtb.ap[:, 0:1])
            m.act(BETA.ap[:, 0:GW], m.bank(bk)[0:16, 0:GW], AF.Sigmoid, [m.pb[bk]], [BETA])
            m.act(e1.ap[:, 0:GW], e1.ap[:, 0:GW], AF.Ln, [e1], [e1], bias=1.0)
            m.P.add("vector", lambda e, GW=GW: e.tensor_tensor_scan(
                out=cs.ap[:, 0:GW], data0=seg.ap[:, 0:GW], data1=e1.ap[:, 0:GW], initial=0.0, op0=ALU.mult, op1=ALU.add),
                [seg, e1], [cs])
            m.ts(Bg, cs.ap[:, 0:GW], al.ap[:, 1:2], ALU.mult, [cs, al], [B])
            m.ts(NB.ap[:, tok0:tok0 + GW], Bg, -1.0, ALU.mult, [B], [NB])
            m.act(EB.ap[:, 0:GW], Bg, AF.Exp, [B], [EB])
            Bv = Bg.rearrange("p (c s) -> p c s", s=64)
            m.cp(bl.ap[:, 0:nch], Bv[:, :, 63], [B], [bl])
            m.act(EL16.ap[:, ch0:ch0 + nch], bl.ap[:, 0:nch], AF.Exp, [bl], [EL16])
            m.tt(tmp.ap[:, 0:GW].rearrange("p (c s) -> p c s", s=64), bl.ap[:, 0:nch].unsqueeze(2).to_broadcast([16, nch, 64]), Bv,
                 ALU.subtract, [bl, B], [tmp])
            m.act(EDEC.ap[:, 0:GW], tmp.ap[:, 0:GW], AF.Exp, [tmp], [EDEC])
            pt5 = m.bank(7)[:, 0:nt * 48].rearrange("p (t f) -> p t f", t=nt)
            for ti in range(nt):
                for (src, off) in ((EB, 0), (BETA, 16), (EDEC, 32)):
                    m.tr(pt5[:, ti, off:off + 16], src.ap[:, ti * 128:(ti + 1) * 128], identf[0:16, 0:16], [src, m.cf], [m.pb[7]])
            TOK = toks[gi % 2]
            m.cp(TOK.ap[:, 0:nt, 0:8], pt5[:, :, 24:32], [m.pb[7]], [TOK])
            m.tt(TOK.ap[:, 0:nt, 8:16], pt5[:, :, 0:8], TOK.ap[:, 0:nt, 0:8], ALU.mult, [m.pb[7], TOK], [TOK])
            m.cp(TOK.ap[:, 0:nt, 16:24], pt5[:, :, 32:40], [m.pb[7]], [TOK])
            m.ts(TOK.ap[:, 0:nt, 24:32], TOK.ap[:, 0:nt, 0:8], -1.0, ALU.mult, [TOK], [TOK])
            m.st(m.TOKD[t0:t0 + nt].rearrange("t p f -> p t f"), TOK, TOK.ap[:, 0:nt, :])
            m.cp(elx.ap[:, 0:nch, :], EL16.ap[:, ch0:ch0 + nch].unsqueeze(2).to_broadcast([16, nch, 128]), [EL16], [elx])
            for cc in range(nch):
                m.mm(m.bank(7)[:, 256 + cc * 8:256 + (cc + 1) * 8], lhsT=elx.ap[:, cc, :], rhs=identf[0:16, 0:8], start=True, stop=True,
                     reads=[elx, m.cf], writes=[m.pb[7]])
            m.cp(ELB.ap[:, ch0:ch0 + nch, :], m.bank(7)[:, 256:256 + nch * 8].rearrange("p (c h) -> p c h", h=8), [m.pb[7]], [ELB])
            for c in range(24):
                typ, h = c // 8, c % 8
                bk = fjob(c * 128, 128)
                xp, sil = xps[kj % 3], sils[kj % 3]
                dwt = dwts[kj % 2]
                sq, sd = sqs[kj % 3], sds[kj % 2]
                cvb = 3 if kj % 2 == 0 else 5
                ssb = 2 if kj % 2 == 0 else 6
                kj += 1
                m.act(xp.ap[:, 3:3 + GW], m.bank(bk)[:, 0:GW], AF.Copy, [m.pb[bk]], [xp])
                m.cp(xp.ap[:, 0:3], carry.ap[:, c, 0:3], [carry], [xp])
                m.tt(dwt.ap, m.identb.unsqueeze(1).to_broadcast([128, 4, 128]), cw.ap[:, c, :].unsqueeze(2).to_broadcast([128, 4, 128]),
                     ALU.mult, [m.cb, cw], [dwt])
                for jj in range(4):
                    m.mm(m.bank(cvb)[:, 0:GW], lhsT=dwt.ap[:, jj, :], rhs=xp.ap[:, jj:jj + GW], start=(jj == 0), stop=(jj == 3),
                         reads=[dwt, xp], writes=[m.pb[cvb]])
                m.cp(carry.ap[:, c, 0:3], xp.ap[:, GW:GW + 3], [xp], [carry])
                if typ == 2:
                    vb = outb[ko % 4]
                    ko += 1
                    m.act(vb.ap[:, 0:GW], m.bank(cvb)[:, 0:GW], AF.Silu, [m.pb[cvb]], [vb])
                    tok_major(vb, m.VS, h)
                    continue
                m.act(sil.ap[:, 0:GW], m.bank(cvb)[:, 0:GW], AF.Silu, [m.pb[cvb]], [sil])
                m.act(sq.ap[:, 0:GW], sil.ap[:, 0:GW], AF.Square, [sil], [sq])
                m.mm(m.bank(ssb)[:, 0:GW], lhsT=onesb.ap, rhs=sq.ap[:, 0:GW], start=True, stop=True, reads=[onesb, sq], writes=[m.pb[ssb]])
                m.act(sd.ap[:, 0:GW], m.bank(ssb)[:, 0:GW], AF.Sqrt, [m.pb[ssb]], [sd], bias=EPS)
                m.recip(sd.ap[:, 0:GW], sd.ap[:, 0:GW], [sd], [sd])
                if typ == 1:
                    kb = outb[ko % 4]
                    ko += 1
                    m.tt(kb.ap[:, 0:GW], sil.ap[:, 0:GW], sd.ap[:, 0:GW], ALU.mult, [sil, sd], [kb])
                    m.st(m.KG[h, :, tok0:tok0 + GW], kb, kb.ap[:, 0:GW])
                    tok_major(kb, m.KT, h)
                else:
                    qb = outb[ko % 4]
                    qe = outb[(ko + 1) % 4]
                    ko += 2
                    m.ts(sd.ap[:, 0:GW], sd.ap[:, 0:GW], 128 ** -0.5, ALU.mult, [sd], [sd])
                    m.tt(qb.ap[:, 0:GW], sil.ap[:, 0:GW], sd.ap[:, 0:GW], ALU.mult, [sil, sd], [qb])
                    m.st(m.QG[h, :, tok0:tok0 + GW], qb, qb.ap[:, 0:GW])
                    m.mm(m.bank(ssb)[:, 0:GW], lhsT=sel.ap[:, h, :], rhs=EB.ap[:, 0:GW], start=True, stop=True, reads=[sel, EB], writes=[m.pb[ssb]])
                    r3 = r3s[0]
                    m.tt(r3.ap[:, 0:GW], sd.ap[:, 0:GW], m.bank(ssb)[:, 0:GW], ALU.mult, [sd, m.pb[ssb]], [r3])
                    m.tt(qe.ap[:, 0:GW], sil.ap[:, 0:GW], r3.ap[:, 0:GW], ALU.mult, [sil, r3], [qe])
                    m.st(m.QE[h, :, tok0:tok0 + GW], qe, qe.ap[:, 0:GW])
            for ti in range(nt):
                t = t0 + ti
                go = gos[kt % 2]
                kt += 1
                for nh in range(2):
                    for kc in range(8):
                        m.mm(m.bank(7), lhsT=ytg.ap[:, ti, kc, :], rhs=W.ap[:, kc, 3072 + nh * 512:3072 + (nh + 1) * 512],
                             start=(kc == 0), stop=(kc == 7), reads=[ytg, m.wblk(WB, 3072 + nh * 512)], writes=[m.pb[7]])
                    m.act(go.ap[:, nh * 512:(nh + 1) * 512], m.bank(7), AF.Silu, [m.pb[7]], [go])
                m.st(m.GS[t], go, go.ap)
        m.barrier()
        m.release(mk)
        if m.stop == (2, "m1"):
            m.release(mk0)
            return "stop"
        G2 = [(g * 2, 2) for g in range(16)] + [(32, 1)]
        TOKf = m.TOKD.rearrange("t p f -> (t p) f")
        KTf = m.KT.rearrange("t p f -> (t p) f")
        VSf = m.VS.rearrange("t p f -> (t p) f")
        GSf = m.GS.rearrange("t p f -> (t p) f")
        OMf = m.OM.rearrange("t p f -> (t p) f")
        mk = m.mark()
        negm = m.tile(64, [2, 64], F32, "negm", dma=True)
        m.ld(negm, negm.ap, m.gnegm.rearrange("a p f -> p a f"))
        I64 = identf[0:64, 0:64]
        I64b = I64.unsqueeze(1).to_broadcast([64, 8, 64])
        gsets = [{"kg": m.tile(128, [8, 256], BF16, "kg", dma=True), "qg": m.tile(128, [8, 256], BF16, "qg", dma=True),
                  "tok": m.tile(64, [4, 32], F32, "tok", dma=True)} for _ in range(3)]
        csets = [{n: m.tile(64, [8, 64], F32, n) for n in ("GL", "GU", "P0", "Pt0", "P1", "Pt1", "Xt")} for _ in range(2)]
        for C_ in csets:
            C_["Bd"] = m.tile(16, [8, 64], F32, "Bd")
            C_["NBd"] = m.tile(16, [8, 64], F32, "NBd")
        E8 = sel.ap[:, 0:8, 0:64]
        ones16 = m.tile(16, [64], F32, "ones16")
        m.memset(ones16.ap, 1.0, [ones16])
        negm8 = m.tile(64, [2, 8, 64], F32, "negm8")
        for a_ in range(2):
            m.cp(negm8.ap[:, a_, :, :], negm.ap[:, a_, :].unsqueeze(1).to_broadcast([64, 8, 64]), [negm], [negm8])
        xas = [m.tile(64, [2, 8, 64], BF16, "xa", dma=True) for _ in range(2)]

        def loads_a(gi):
            t0, nt = G2[gi]
            g = gsets[gi % 3]
            GW = nt * 128
            tok0 = t0 * 128
            m.ld(g["kg"], g["kg"].ap[:, :, 0:GW], m.KG[:, :, tok0:tok0 + GW].rearrange("h d t -> d h t"))
            m.ld(g["qg"], g["qg"].ap[:, :, 0:GW], m.QG[:, :, tok0:tok0 + GW].rearrange("h d t -> d h t"))
            m.ld(g["tok"], g["tok"].ap[:, 0:2 * nt, :], TOKf[tok0:tok0 + GW, :].rearrange("(c s) f -> s c f", s=64))

        def chunk_a(c):
            gi, cc = c // 4, c % 4
            g = gsets[gi % 3]
            kg, qg, tok = g["kg"], g["qg"], g["tok"]
            C = csets[c % 2]
            xa = xas[c % 2]
            bb = 4 * (c % 2)
            tsl = slice(c * 64, (c + 1) * 64)
            csl = slice(cc * 64, (cc + 1) * 64)

            def bk3(i):
                return m.bank(bb + i)[0:64, :].rearrange("p (h s) -> p h s", h=8)

            Bd, NBd = C["Bd"], C["NBd"]
            m.tt(Bd.ap, B.ap[:, tsl].unsqueeze(1).to_broadcast([16, 8, 64]), E8, ALU.mult, [B, sel], [Bd])
            m.tt(NBd.ap, NB.ap[:, tsl].unsqueeze(1).to_broadcast([16, 8, 64]), E8, ALU.mult, [NB, sel], [NBd])
            o = m.bank(bb)[0:64, :]
            m.mm(o, lhsT=B.ap[:, tsl], rhs=E8, start=True, stop=False, reads=[B, sel], writes=[m.pb[bb]])
            m.mm(o, lhsT=ones16.ap, rhs=NBd.ap, start=False, stop=False, reads=[ones16, NBd], writes=[m.pb[bb]])
            m.mm(o, lhsT=I64, rhs=negm8.ap[:, 0, :, :], start=False, stop=True, reads=[m.cf, negm8], writes=[m.pb[bb]])
            o = m.bank(bb + 1)[0:64, :]
            m.mm(o, lhsT=ones16.ap, rhs=Bd.ap, start=True, stop=False, reads=[ones16, Bd], writes=[m.pb[bb + 1]])
            m.mm(o, lhsT=NB.ap[:, tsl], rhs=E8, start=False, stop=False, reads=[NB, sel], writes=[m.pb[bb + 1]])
            m.mm(o, lhsT=I64, rhs=negm8.ap[:, 1, :, :], start=False, stop=True, reads=[m.cf, negm8], writes=[m.pb[bb + 1]])
            for h in range(8):
                m.mm(bk3(2)[:, h, :], lhsT=kg.ap[:, h, csl], rhs=kg.ap[:, h, csl], start=True, stop=True, reads=[kg], writes=[m.pb[bb + 2]])
            for h in range(8):
                m.mm(bk3(3)[:, h, :], lhsT=kg.ap[:, h, csl], rhs=qg.ap[:, h, csl], start=True, stop=True, reads=[kg, qg], writes=[m.pb[bb + 3]])
            m.act(C["GL"].ap, bk3(0), AF.Exp, [m.pb[bb]], [C["GL"]])
            m.act(C["GU"].ap, bk3(1), AF.Exp, [m.pb[bb + 1]], [C["GU"]])
            yield
            m.tt(C["P0"].ap, bk3(2), C["GL"].ap, ALU.mult, [m.pb[bb + 2], C["GL"]], [C["P0"]])
            m.tt(C["P0"].ap, C["P0"].ap, tok.ap[:, cc, 24:32].unsqueeze(2).to_broadcast([64, 8, 64]), ALU.mult, [C["P0"], tok], [C["P0"]])
            m.tt(xa.ap[:, 1, :, :], bk3(3), C["GU"].ap, ALU.mult, [m.pb[bb + 3], C["GU"]], [xa])
            for h in range(8):
                m.tr(bk3(0)[:, h, :], C["P0"].ap[:, h, :], I64, [C["P0"], m.cf], [m.pb[bb]])
            m.act(C["Pt0"].ap, bk3(0), AF.Copy, [m.pb[bb]], [C["Pt0"]])
            yield
            m.tt(C["Xt"].ap, C["Pt0"].ap, I64b, ALU.add, [C["Pt0"], m.cf], [C["Xt"]])
            Pp, Ptp = C["P0"], C["Pt0"]
            for it in range(1, 6):
                Pn, Ptn = (C["P1"], C["Pt1"]) if it % 2 == 1 else (C["P0"], C["Pt0"])
                for h in range(8):
                    m.mm(bk3(1)[:, h, :], lhsT=Ptp.ap[:, h, :], rhs=Pp.ap[:, h, :], start=True, stop=True, reads=[Ptp, Pp], writes=[m.pb[bb + 1]])
                if it < 5:
                    for h in range(8):
                        m.mm(bk3(2)[:, h, :], lhsT=Pp.ap[:, h, :], rhs=Ptp.ap[:, h, :], start=True, stop=True, reads=[Ptp, Pp], writes=[m.pb[bb + 2]])
                m.act(Pn.ap, bk3(1), AF.Copy, [m.pb[bb + 1]], [Pn])
                if it < 5:
                    m.cp(Ptn.ap, bk3(2), [m.pb[bb + 2]], [Ptn])
                yield
                for h in range(8):
                    m.mm(bk3(3)[:, h, :], lhsT=Pn.ap[:, h, :], rhs=C["Xt"].ap[:, h, :], start=True, stop=True, reads=[Pn, C["Xt"]], writes=[m.pb[bb + 3]])
                m.tt(C["Xt"].ap, C["Xt"].ap, bk3(3), ALU.add, [C["Xt"], m.pb[bb + 3]], [C["Xt"]])
                Pp, Ptp = Pn, Ptn
                yield
            m.act(xa.ap[:, 0, :, :], C["Xt"].ap, AF.Copy, [C["Xt"]], [xa])
            m.st(m.XAD[c], xa, xa.ap.rearrange("p a h s -> p (a h s)"))

        loads_a(0)
        active = []
        nxt = 0
        while nxt < 66 or active:
            while len(active) < 2 and nxt < 66:
                if nxt % 4 == 0 and nxt // 4 + 1 < len(G2):
                    loads_a(nxt // 4 + 1)
                active.append(chunk_a(nxt))
                nxt += 1
            for gen in list(active):
                try:
                    next(gen)
                except StopIteration:
                    active.remove(gen)
        m.barrier()
        m.release(mk)
        if m.stop == (2, "m2a"):
            m.release(mk0)
            return "stop"
        mk = m.mark()
        og = m.tile(64, [D], F32, "og", dma=True)
        m.ld(og, og.ap, m.gdn_og[j])
        Ss = [m.tile(128, [4, 128], F32, "S") for _ in range(2)]
        Sbfs = [[m.tile(128, [4, 128], BF16, "Sbf") for _ in range(2)] for _ in range(2)]
        for hh in range(2):
            m.memset(Ss[hh].ap, 0.0, [Ss[hh]])
            m.memset(Sbfs[hh][1].ap, 0.0, [Sbfs[hh][1]])
        bsets = [{"kg": m.tile(128, [8, 256], BF16, "kg", dma=True), "qe": m.tile(128, [8, 256], BF16, "qe", dma=True),
                  "kt": m.tile(64, [4, D], BF16, "kt", dma=True), "vs": m.tile(64, [4, D], BF16, "vs", dma=True),
                  "gs": m.tile(64, [4, D], BF16, "gs", dma=True), "tok": m.tile(64, [4, 32], F32, "tok", dma=True),
                  "xa": m.tile(64, [4, 2, 512], BF16, "xa", dma=True)} for _ in range(2)]
        bvs = [[m.tile(64, [4, 128], F32, "bv") for _ in range(2)] for _ in range(2)]
        t1s = [m.tile(64, [4, 128], F32, "t1") for _ in range(2)]
        Rs = [m.tile(64, [4, 128], BF16, "R") for _ in range(2)]
        vns = [m.tile(64, [4, 128], BF16, "vn") for _ in range(2)]
        vds = [m.tile(64, [4, 128], BF16, "vd") for _ in range(2)]
        junks = [m.tile(64, [128], BF16, "junk") for _ in range(2)]
        sss = [[m.tile(64, [8], F32, "ss") for _ in range(2)] for _ in range(2)]
        ons = [[m.tile(64, [512], F32, "on") for _ in range(2)] for _ in range(2)]
        oms = [[m.tile(64, [512], BF16, "om", dma=True) for _ in range(2)] for _ in range(2)]

        def loads_b(gi):
            t0, nt = G2[gi]
            g = bsets[gi % 2]
            GW = nt * 128
            tok0 = t0 * 128
            nch = 2 * nt
            c0 = 2 * t0
            m.ld(g["kg"], g["kg"].ap[:, :, 0:GW], m.KG[:, :, tok0:tok0 + GW].rearrange("h d t -> d h t"))
            m.ld(g["qe"], g["qe"].ap[:, :, 0:GW], m.QE[:, :, tok0:tok0 + GW].rearrange("h d t -> d h t"))
            m.ld(g["kt"], g["kt"].ap[:, 0:nch, :], KTf[tok0:tok0 + GW, :].rearrange("(c s) f -> s c f", s=64))
            m.ld(g["vs"], g["vs"].ap[:, 0:nch, :], VSf[tok0:tok0 + GW, :].rearrange("(c s) f -> s c f", s=64))
            m.ld(g["gs"], g["gs"].ap[:, 0:nch, :], GSf[tok0:tok0 + GW, :].rearrange("(c s) f -> s c f", s=64))
            m.ld(g["tok"], g["tok"].ap[:, 0:nch, :], TOKf[tok0:tok0 + GW, :].rearrange("(c s) f -> s c f", s=64))
            m.ld(g["xa"], g["xa"].ap[:, 0:nch, :, :].rearrange("p c a f -> p c (a f)"), m.XAD[c0:c0 + nch].rearrange("c s f -> s c f"))

        loads_b(0)
        for c in range(66):
            gi, cc = c // 4, c % 4
            if cc == 0 and gi + 1 < len(G2):
                loads_b(gi + 1)
            g = bsets[gi % 2]
            kg, qe, kt, vs, gs, tok, xa = g["kg"], g["qe"], g["kt"], g["vs"], g["gs"], g["tok"], g["xa"]
            csl = slice(cc * 64, (cc + 1) * 64)
            for hh in range(2):
                h0 = 4 * hh
                S = Ss[hh]
                Sp, Sn = Sbfs[hh][(c + 1) % 2], Sbfs[hh][c % 2]
                bv = bvs[hh][c % 2]
                t1, R, vn, vd, junk = t1s[hh], Rs[hh], vns[hh], vds[hh], junks[hh]
                fsl = slice(hh * 512, (hh + 1) * 512)

                def bc4(col0, P=64, W_=128):
                    return tok.ap[:, cc, col0 + h0:col0 + h0 + 4].unsqueeze(2).to_broadcast([P, 4, W_])

                def pk(b0, P=64):
                    return m.bank(b0 + hh)[0:P, :].rearrange("p (h v) -> p h v", h=4)

                m.tt(bv.ap, vs.ap[:, cc, fsl].rearrange("p (h v) -> p h v", h=4), bc4(0), ALU.mult, [vs, tok], [bv])
                for hl in range(4):
                    h = h0 + hl
                    m.mm(pk(0)[:, hl, :], lhsT=kg.ap[:, h, csl], rhs=Sp.ap[:, hl, :], start=True, stop=True, reads=[kg, Sp], writes=[m.pb[0 + hh]])
                m.tt(t1.ap, pk(0), bc4(8), ALU.mult, [m.pb[0 + hh], tok], [t1])
                m.tt(R.ap, bv.ap, t1.ap, ALU.subtract, [bv, t1], [R])
                for hl in range(4):
                    h = h0 + hl
                    m.mm(pk(2)[:, hl, :], lhsT=xa.ap[:, cc, 0, h * 64:(h + 1) * 64], rhs=R.ap[:, hl, :], start=True, stop=True,
                         reads=[xa, R], writes=[m.pb[2 + hh]])
                m.act(vn.ap, pk(2), AF.Copy, [m.pb[2 + hh]], [vn])
                m.tt(vd.ap, pk(2), bc4(16), ALU.mult, [m.pb[2 + hh], tok], [vd])
                for hl in range(4):
                    h = h0 + hl
                    m.mm(pk(4)[:, hl, :], lhsT=xa.ap[:, cc, 1, h * 64:(h + 1) * 64], rhs=vn.ap[:, hl, :], start=True, stop=False,
                         reads=[xa, vn], writes=[m.pb[4 + hh]])
                    m.mm(pk(4)[:, hl, :], lhsT=qe.ap[:, h, csl], rhs=Sp.ap[:, hl, :], start=False, stop=True,
                         reads=[qe, Sp], writes=[m.pb[4 + hh]])
                for hl in range(4):
                    h = h0 + hl
                    m.mm(pk(6, 128)[:, hl, :], lhsT=kt.ap[:, cc, h * 128:(h + 1) * 128], rhs=vd.ap[:, hl, :], start=True, stop=True,
                         reads=[kt, vd], writes=[m.pb[6 + hh]])
                m.tt(S.ap, S.ap, ELB.ap[:, c, h0:h0 + 4].unsqueeze(2).to_broadcast([128, 4, 128]), ALU.mult, [S, ELB], [S])
                m.tt(S.ap, S.ap, pk(6, 128), ALU.add, [S, m.pb[6 + hh]], [S])
                m.act(Sn.ap, S.ap, AF.Copy, [S], [Sn])
                ss = sss[hh][c % 2]
                on = ons[hh][c % 2]
                om = oms[hh][c % 2]
                for hl in range(4):
                    m.act(junk.ap, pk(4)[:, hl, :], AF.Square, [m.pb[4 + hh]], [junk, ss], accum=ss.ap[:, hl:hl + 1])
                m.act(ss.ap[:, 4:8], ss.ap[:, 0:4], AF.Sqrt, [ss], [ss], scale=1.0 / 128, bias=EPS)
                m.recip(ss.ap[:, 4:8], ss.ap[:, 4:8], [ss], [ss])
                m.tt(on.ap.rearrange("p (h v) -> p h v", h=4), pk(4), ss.ap[:, 4:8].unsqueeze(2).to_broadcast([64, 4, 128]), ALU.mult,
                     [m.pb[4 + hh], ss], [on])
                m.tt(on.ap, on.ap, og.ap[:, fsl], ALU.mult, [on, og], [on], eng="gpsimd")
                m.tt(om.ap, on.ap, gs.ap[:, cc, fsl], ALU.mult, [on, gs], [om], eng="gpsimd")
                m.st(OMf[c * 64:(c + 1) * 64, fsl], om, om.ap)
        m.barrier()
        m.release(mk)
        m.release(mk0)


def _consts():
    p = np.arange(128)[:, None]
    f = np.arange(128)[None, :]
    I = (p == f)
    U = (p <= f)
    SU = (p < f)
    L = (p >= f)
    SL = (p > f)
    BD = ((p // 64) == (f // 64))
    return np.stack([I, U, SU, L, SL, BD]).astype(np.float32)


def _pad16(v):
    o = np.zeros((v.shape[0], 16, 1), np.float32)
    o[:, 0:8, 0] = v
    return o


def _gconst():
    g = np.zeros((16, 16, 128), np.float32)
    for r in range(16):
        g[r, r, :] = 1.0
    return g


def _gnegm():
    p = np.arange(64)[:, None]
    f = np.arange(64)[None, :]
    lo = np.where(p > f, 0.0, -30000.0)
    up = np.where(p <= f, 0.0, -30000.0)
    return np.stack([lo, up]).astype(np.float32)


def _tile_gain(g):
    L = g.shape[0]
    return np.ascontiguousarray(g.reshape(L, 8, 128).transpose(0, 2, 1))


def prep_inputs(inputs, b):
    x = np.asarray(inputs["x"])
    meta = np.asarray(inputs["meta_tokens"])
    h0 = np.zeros((T, D), np.float32)
    h0[0:16] = meta
    h0[16:16 + 4096] = x[b]
    mp = {
        "h0": h0,
        "consts": _consts(),
        "norm_mix_t": _tile_gain(np.asarray(inputs["norm_mix"])),
        "norm_ffn_t": _tile_gain(np.asarray(inputs["norm_ffn"])),
        "w_gate_up": np.asarray(inputs["w_gate_up"]),
        "w_down": np.asarray(inputs["w_down"]),
        "fox_w_in": np.asarray(inputs["fox_w_in"]),
        "fox_bf": np.ascontiguousarray(np.asarray(inputs["fox_b_f"])[:, :, None]),
        "fox_qg": np.ascontiguousarray(np.tile(np.asarray(inputs["fox_q_gain"]), (1, 2))[:, :, None]),
        "fox_kg": np.ascontiguousarray(np.tile(np.asarray(inputs["fox_k_gain"]), (1, 2))[:, :, None]),
        "fox_w_out": np.asarray(inputs["fox_w_out"]),
        "gla_w_in": np.asarray(inputs["gla_w_in"]),
        "gla_w_alpha2": np.asarray(inputs["gla_w_alpha2"]),
        "gla_ba": np.ascontiguousarray(np.asarray(inputs["gla_b_alpha"]).reshape(1, 4, 128).transpose(0, 2, 1)),
        "gla_og": np.ascontiguousarray(np.broadcast_to(np.tile(np.asarray(inputs["gla_o_gain"]), (1, 4))[:, None, :], (1, 64, D))),
        "gla_w_out": np.asarray(inputs["gla_w_out"]),
        "gdn_w_in": np.asarray(inputs["gdn_w_in"]),
        "gdn_cw": np.ascontiguousarray(np.asarray(inputs["gdn_conv_w"]).reshape(1, 4, 24, 128).transpose(0, 3, 2, 1)),
        "gdn_alog": _pad16(np.asarray(inputs["gdn_a_log"])),
        "gdn_dtb": _pad16(np.asarray(inputs["gdn_dt_bias"])),
        "gdn_og": np.ascontiguousarray(np.broadcast_to(np.tile(np.asarray(inputs["gdn_o_gain"]), (1, 8))[:, None, :], (1, 64, D))),
        "gdn_w_out": np.asarray(inputs["gdn_w_out"]),
        "gconst": _gconst(),
        "gnegm": _gnegm(),
    }
    return mp


_NC_CACHE = {}


def kernel(**inputs):
    if "nc" not in _NC_CACHE:
        _NC_CACHE["nc"] = MK(4).build()
    nc = _NC_CACHE["nc"]
    in_maps = [prep_inputs(inputs, b) for b in range(8)]
    res = run_bass_kernel_spmd(nc, in_maps, core_ids=list(range(8)))
    return np.stack([r["out"] for r in res.results], axis=0).astype(np.float32)
```
